# Optimizing a Trainium2 kernel written in Bass

```python
import math
import jax, jax.numpy as jnp
from jax import lax
import numpy as np

D_MODEL = 4096
BATCH = 2
SEQ = 8192
DEPTH = 1

A_WIDTH = D_MODEL // 2
A_CHUNK = 128
A_HEADS = 16
A_HEAD_DIM = A_WIDTH // A_HEADS
B_WIDTH = D_MODEL // 4
S5_GROUP_DIM = 16
S5_GROUPS = B_WIDTH // S5_GROUP_DIM
S5_STATE = 64
DT_MIN = 1e-3
DT_MAX = 1e-1
D_FF = 4 * D_MODEL
NORM_EPS = 1e-6
IN_WIDTH = 2 * A_WIDTH + B_WIDTH + 2 * D_MODEL

kernel_name = "hybrid_gmlp_s5_gated_block"


def rms_norm(x, g):
    xf = x.astype(jnp.float32)
    y = xf * lax.rsqrt(jnp.mean(xf * xf, axis=-1, keepdims=True) + NORM_EPS)
    return (y * g.astype(jnp.float32)).astype(x.dtype)


def layer_norm(x, g, b):
    xf = x.astype(jnp.float32)
    mu = jnp.mean(xf, axis=-1, keepdims=True)
    xc = xf - mu
    y = xc * lax.rsqrt(jnp.mean(xc * xc, axis=-1, keepdims=True) + NORM_EPS)
    return (y * g.astype(jnp.float32) + b.astype(jnp.float32)).astype(x.dtype)


def chunked_spatial_gating(u, v, w_s, b_s):
    bt, seq_len, _ = u.shape
    n_chunks = seq_len // A_CHUNK
    v = v.reshape(bt, n_chunks, A_CHUNK, A_HEADS, A_HEAD_DIM)
    causal = jnp.tril(jnp.ones((A_CHUNK, A_CHUNK), dtype=bool))
    w = jnp.where(causal[None], w_s, jnp.zeros_like(w_s))
    mixed = jnp.einsum('hts,bcshe->bcthe', w, v) + b_s.T[None, None, :, :, None]
    return u * mixed.reshape(bt, seq_len, A_WIDTH)


def s5_layer(u, lam_re, lam_im, log_dt, b_re, b_im, c_re, c_im, d_skip):
    f32 = jnp.float32
    bt, seq_len, _ = u.shape
    uf = u.astype(f32).reshape(bt, seq_len, S5_GROUPS, S5_GROUP_DIM)
    lam = lax.complex(lam_re.astype(f32), lam_im.astype(f32))
    dt = jnp.exp(log_dt.astype(f32))[:, None]
    a_bar = jnp.exp(lam * dt)
    b_mat = lax.complex(b_re.astype(f32), b_im.astype(f32))
    b_bar = ((a_bar - 1.0) / lam)[..., None] * b_mat
    c_mat = lax.complex(c_re.astype(f32), c_im.astype(f32))
    bu = jnp.einsum('gph,blgh->blgp', b_bar, uf.astype(jnp.complex64))
    a_seq = jnp.broadcast_to(a_bar, bu.shape)

    def combine(left, right):
        a_l, b_l = left
        a_r, b_r = right
        return a_r * a_l, a_r * b_l + b_r

    _, states = lax.associative_scan(combine, (a_seq, bu), axis=1)
    y = jnp.einsum('ghp,blgp->blgh', c_mat, states).real
    y = y + d_skip.astype(f32).reshape(S5_GROUPS, S5_GROUP_DIM) * uf
    return y.reshape(bt, seq_len, B_WIDTH).astype(u.dtype)


def setup_inputs(seed: int = 0) -> dict:
    key = jax.random.key(seed)
    ks = jax.random.split(key, 32)
    f32 = jnp.float32
    L = DEPTH

    def nrm(k, shape, scale):
        return jax.random.normal(k, shape, f32) * scale

    def gain(k, shape):
        return 1.0 + 0.02 * jax.random.normal(k, shape, f32)

    x = jax.random.normal(ks[0], (BATCH, SEQ, D_MODEL), f32)
    norm_mix_pre = gain(ks[1], (L, D_MODEL))
    w_in = nrm(ks[2], (L, D_MODEL, IN_WIDTH), D_MODEL ** -0.5)
    v_norm_g = gain(ks[3], (L, A_WIDTH))
    v_norm_b = nrm(ks[4], (L, A_WIDTH), 0.02)
    w_spatial = nrm(ks[5], (L, A_HEADS, A_CHUNK, A_CHUNK), 0.5 * A_CHUNK ** -0.5)
    b_spatial = gain(ks[6], (L, A_HEADS, A_CHUNK))
    w_proj_a = nrm(ks[7], (L, A_WIDTH, D_MODEL), A_WIDTH ** -0.5)
    n = jnp.arange(S5_STATE, dtype=f32)
    lam_re = -0.5 + 0.01 * jax.random.normal(ks[8], (L, S5_GROUPS, S5_STATE), f32)
    lam_im = math.pi * n + 0.01 * jax.random.normal(ks[9], (L, S5_GROUPS, S5_STATE), f32)
    log_dt = jax.random.uniform(ks[10], (L, S5_GROUPS), f32, math.log(DT_MIN), math.log(DT_MAX))
    b_re = nrm(ks[11], (L, S5_GROUPS, S5_STATE, S5_GROUP_DIM), (2 * S5_GROUP_DIM) ** -0.5)
    b_im = nrm(ks[12], (L, S5_GROUPS, S5_STATE, S5_GROUP_DIM), (2 * S5_GROUP_DIM) ** -0.5)
    c_re = nrm(ks[13], (L, S5_GROUPS, S5_GROUP_DIM, S5_STATE), (2 * S5_STATE) ** -0.5)
    c_im = nrm(ks[14], (L, S5_GROUPS, S5_GROUP_DIM, S5_STATE), (2 * S5_STATE) ** -0.5)
    d_skip = nrm(ks[15], (L, B_WIDTH), 1.0)
    w_glu_a = nrm(ks[16], (L, B_WIDTH, D_MODEL), B_WIDTH ** -0.5)
    w_glu_b = nrm(ks[17], (L, B_WIDTH, D_MODEL), B_WIDTH ** -0.5)
    w_out = nrm(ks[18], (L, D_MODEL, D_MODEL), D_MODEL ** -0.5)
    norm_mix_post = gain(ks[19], (L, D_MODEL))
    norm_mlp_pre = gain(ks[20], (L, D_MODEL))
    w_ff_up = nrm(ks[21], (L, D_MODEL, D_FF), D_MODEL ** -0.5)
    w_ff_down = nrm(ks[22], (L, D_FF, D_MODEL), D_FF ** -0.5)
    norm_mlp_post = gain(ks[23], (L, D_MODEL))
    return {"x": x, "norm_mix_pre": norm_mix_pre, "w_in": w_in, "v_norm_g": v_norm_g,
            "v_norm_b": v_norm_b, "w_spatial": w_spatial, "b_spatial": b_spatial,
            "w_proj_a": w_proj_a, "lam_re": lam_re, "lam_im": lam_im, "log_dt": log_dt,
            "b_re": b_re, "b_im": b_im, "c_re": c_re, "c_im": c_im, "d_skip": d_skip,
            "w_glu_a": w_glu_a, "w_glu_b": w_glu_b, "w_out": w_out,
            "norm_mix_post": norm_mix_post, "norm_mlp_pre": norm_mlp_pre,
            "w_ff_up": w_ff_up, "w_ff_down": w_ff_down, "norm_mlp_post": norm_mlp_post}


def reference(x, norm_mix_pre, w_in, v_norm_g, v_norm_b, w_spatial, b_spatial, w_proj_a,
              lam_re, lam_im, log_dt, b_re, b_im, c_re, c_im, d_skip, w_glu_a, w_glu_b,
              w_out, norm_mix_post, norm_mlp_pre, w_ff_up, w_ff_down, norm_mlp_post):
    for l in range(DEPTH):
        h = rms_norm(x, norm_mix_pre[l])
        proj = h @ w_in[l]
        o1 = 2 * A_WIDTH
        o2 = o1 + B_WIDTH
        o3 = o2 + D_MODEL
        z_a = jax.nn.gelu(proj[..., :o1])
        u_a, v_a = z_a[..., :A_WIDTH], z_a[..., A_WIDTH:]
        x_b = proj[..., o1:o2]
        gate_a = jax.nn.sigmoid(proj[..., o2:o3])
        gate_b = jax.nn.sigmoid(proj[..., o3:])
        v_a = layer_norm(v_a, v_norm_g[l], v_norm_b[l])
        s_a = chunked_spatial_gating(u_a, v_a, w_spatial[l], b_spatial[l])
        branch_a = s_a @ w_proj_a[l]
        y_b = s5_layer(x_b, lam_re[l], lam_im[l], log_dt[l], b_re[l], b_im[l],
                       c_re[l], c_im[l], d_skip[l])
        z_b = jax.nn.gelu(y_b)
        branch_b = (z_b @ w_glu_a[l]) * jax.nn.sigmoid(z_b @ w_glu_b[l])
        mix = (gate_a * branch_a + gate_b * branch_b) @ w_out[l]
        x = x + rms_norm(mix, norm_mix_post[l])
        h = rms_norm(x, norm_mlp_pre[l])
        ff = jnp.square(jax.nn.relu(h @ w_ff_up[l])) @ w_ff_down[l]
        x = x + rms_norm(ff, norm_mlp_post[l])
    return x
```

```python
import numpy as np
import concourse.bass as bass
import concourse.mybir as mybir
from concourse.bass_utils import run_bass_kernel_spmd

F32 = mybir.dt.float32
BF16 = mybir.dt.bfloat16
AF = mybir.ActivationFunctionType
ALU = mybir.AluOpType
AX = mybir.AxisListType

D = 4096
AW = 2048
BW = 1024
DFF = 16384
TT = 512
OWN = 2048
NCORES = 8
EPS = 1e-6
NBLK = 848
KB = 8
PI = float(np.pi)
GELU_C = 1.5957691216057308


def _es(dt):
    return mybir.dt.size(dt)


class Ins:
    __slots__ = ("eng", "fn", "deps", "dma", "signal", "sigval", "dmaval", "small")


class Prog:
    PAGE = 512
    COMPUTE = ("pe", "act", "dve", "pool")

    def __init__(self, nc):
        self.nc = nc
        self.ins = []
        self.pages = {}
        self.dma_cnt = {}

    def _pages(self, ap):
        t = ap.tensor
        es = _es(ap.dtype)
        pat = ap.ap
        off = ap.offset
        sp = str(ap.space)
        if sp in ("SB", "PSUM"):
            row = _es(t.dtype)
            for s in list(t.shape)[1:]:
                row *= s
            lo = (off * es) % row
            ext = 1
            for st, cnt in pat[1:]:
                ext += (cnt - 1) * abs(st)
            page = 2048 if sp == "PSUM" else self.PAGE
        else:
            lo = off * es
            ext = 1
            for st, cnt in pat:
                ext += (cnt - 1) * abs(st)
            page = 65536
        hi = lo + ext * es
        key = (sp, t.name)
        return [(key, p) for p in range(lo // page, (hi - 1) // page + 1)]

    def add(self, eng, fn, reads=(), writes=(), dma=None, force_sync=False):
        idx = len(self.ins)
        small = force_sync
        for ap in list(writes) + list(reads):
            n = 1
            for st, cnt in ap.ap[1:]:
                n *= cnt
            if n <= 64:
                small = True
        deps = set()
        rp = []
        wp = []
        for ap in reads:
            if str(ap.space) == "PSUM":
                wp.extend(self._pages(ap))
            else:
                rp.extend(self._pages(ap))
        for ap in writes:
            wp.extend(self._pages(ap))
        for pg in rp:
            rec = self.pages.get(pg)
            if rec is not None and rec[0] is not None:
                deps.add(rec[0])
        for pg in wp:
            rec = self.pages.get(pg)
            if rec is not None:
                if rec[0] is not None:
                    deps.add(rec[0])
                deps.update(rec[1].values())
                deps.update(rec[2])
        for pg in rp:
            rec = self.pages.get(pg)
            if rec is None:
                rec = [None, {}, set()]
                self.pages[pg] = rec
            if dma is not None:
                rec[2].add(idx)
            else:
                rec[1][eng] = idx
        for pg in wp:
            self.pages[pg] = [idx, {}, set()]
        i = Ins()
        i.eng = eng
        i.fn = fn
        i.deps = deps
        i.dma = dma
        i.signal = False
        i.sigval = 0
        i.dmaval = 0
        i.small = small
        if dma is not None:
            self.dma_cnt[dma] = self.dma_cnt.get(dma, 0) + 1
            i.dmaval = 16 * self.dma_cnt[dma]
        self.ins.append(i)
        return idx

    def finalize(self, same_engine_sync=True):
        nc = self.nc
        ins = self.ins
        need = []
        for k, i in enumerate(ins):
            w = []
            for d in i.deps:
                di = ins[d]
                if di.dma is not None:
                    w.append(d)
                elif di.eng == i.eng and i.dma is None and (di.eng == "pe" or not same_engine_sync
                                                            or not (di.small or i.small)):
                    continue
                else:
                    di.signal = True
                    w.append(d)
            need.append(w)
        cnt = {e: 0 for e in ("pe", "act", "dve", "pool", "sp")}
        for i in ins:
            if i.dma is None and i.signal:
                cnt[i.eng] += 1
                i.sigval = cnt[i.eng]
        sems = {}
        for e in cnt:
            sems[e] = nc.alloc_semaphore(name=f"sem_{e}")
        dsems = {}
        for s in self.dma_cnt:
            dsems[s] = nc.alloc_semaphore(name=f"dsem_{s}")
        per_eng = {e: [] for e in cnt}
        for k, i in enumerate(ins):
            per_eng[i.eng].append(k)
        last_dma = {}
        prev_same_slot = {}
        for k, i in enumerate(ins):
            if i.dma is not None:
                if i.dma in last_dma:
                    prev_same_slot[k] = last_dma[i.dma]
                last_dma[i.dma] = k

        def emit(engname, engine):
            waited = {}
            for k in per_eng[engname]:
                i = ins[k]
                wl = {}
                for d in need[k]:
                    di = ins[d]
                    if di.dma is not None:
                        key = ("d", di.dma)
                        val = di.dmaval
                    else:
                        key = ("e", di.eng)
                        val = di.sigval
                    if wl.get(key, 0) < val:
                        wl[key] = val
                if k in prev_same_slot:
                    di = ins[prev_same_slot[k]]
                    key = ("d", di.dma)
                    if wl.get(key, 0) < di.dmaval:
                        wl[key] = di.dmaval
                for key, val in wl.items():
                    if waited.get(key, 0) >= val:
                        continue
                    waited[key] = val
                    sem = dsems[key[1]] if key[0] == "d" else sems[key[1]]
                    engine.wait_ge(sem, val)
                inst = i.fn(engine)
                if i.dma is not None:
                    inst.then_inc(dsems[i.dma], 16)
                elif i.signal:
                    inst.then_inc(sems[engname], 1)
            if engname == "sp":
                for s, c in self.dma_cnt.items():
                    if waited.get(("d", s), 0) < 16 * c:
                        engine.wait_ge(dsems[s], 16 * c)

        with nc.Block() as block:
            @block.tensor
            def _(e):
                emit("pe", e)

            @block.scalar
            def _(e):
                emit("act", e)

            @block.vector
            def _(e):
                emit("dve", e)

            @block.gpsimd
            def _(e):
                emit("pool", e)

            @block.sync
            def _(e):
                emit("sp", e)


class Builder:
    def __init__(self, NPRE=12, NT=4, gelu_native=True, NS=4, NB=4, cast_engs=("act", "dve"), stop=None):
        self.stop = stop
        self.NPRE = NPRE
        self.NT = NT
        self.gelu_native = gelu_native
        self.NS = NS
        self.NB = NB
        self.cast_engs = cast_engs
        nc = bass.Bass("TRN2", target_bir_lowering=False)
        self.nc = nc
        self.P = Prog(nc)
        dt = nc.dram_tensor
        self.xo = dt("xo", [NT * TT, D], F32, kind="ExternalInput")
        self.xp = dt("xp", [max(NPRE, 1) * TT, D], F32, kind="ExternalInput")
        self.wall = dt("wall", [NBLK, 128, 2048], F32, kind="ExternalInput")
        self.wxb = dt("wxb", [16, 128, 2048], F32, kind="ExternalInput")
        self.gains = dt("gains", [2, 128, D], F32, kind="ExternalInput")
        self.gcol_d = dt("gcol", [128, 64], F32, kind="ExternalInput")
        self.lngb = dt("lngb", [2, 128, AW], F32, kind="ExternalInput")
        self.wsT_d = dt("wsT", [128, 2048], F32, kind="ExternalInput")
        self.bsp_d = dt("bsp", [128, 2048], F32, kind="ExternalInput")
        self.lam_bc = dt("lam_bc", [3, 128, 4096], F32, kind="ExternalInput")
        self.lamcol_d = dt("lamcol", [128, 96], F32, kind="ExternalInput")
        self.bbd = dt("bbd", [8, 128, 1024], F32, kind="ExternalInput")
        self.cbd = dt("cbd", [8, 128, 1024], F32, kind="ExternalInput")
        self.dcol_d = dt("dcol", [128, 8], F32, kind="ExternalInput")
        self.out = dt("out", [NT * TT, D], F32, kind="ExternalOutput")
        if self.stop:
            self.dbgA = dt("dbgA", [128, 16384], BF16, kind="ExternalOutput")
            self.dbgB = dt("dbgB", [128, 16384], BF16, kind="ExternalOutput")
            self.dbgBIG = dt("dbgBIG", [128, 16384], F32, kind="ExternalOutput")
        self.s5c = dt("s5c", [8, 128, 4096], BF16, kind="Internal")
        self.x1s = dt("x1s", [NT * TT, D], F32, kind="Internal")

        sb = nc.alloc_sbuf_tensor
        self.wst = [sb(f"wst{i}", [128, 2048], F32) for i in range(NS)]
        self.wbf = [sb(f"wbf{i}", [128, KB, 256], BF16) for i in range(NB)]
        self.actA = sb("actA", [128, 32, 512], BF16)
        self.actB = sb("actB", [128, 32, 512], BF16)
        self.BIG = sb("BIG", [128, 16384], F32)
        self.BIGb = self.BIG.bitcast(BF16)
        self.gbc = sb("gbc", [128, 2048], F32)
        self.xbuf = sb("xbuf", [128, 4096], BF16)
        self.xbufF = self.xbuf.bitcast(F32)
        self.ident = sb("ident", [128, 128], BF16)
        self.tri = sb("tri", [128, 128], BF16)
        self.wsT = sb("wsTb", [128, 16, 128], BF16)
        self.trow = sb("trow", [128, 128], F32)
        self.sm = sb("sm", [128, 512], F32)
        ps = nc.alloc_psum_tensor
        self.ps2 = [ps(f"ps{i}", [128, 1024], F32) for i in range(4)]
        self.slot_i = 0
        self.wcount = 0
        self.c_scol = 0
        self.c_nscol = 1
        self.c_stat = 2
        self.c_bn = 20
        self.c_mv = 70
        self.c_ms = 72
        self.c_gcol = 80
        self.c_dcol = 144
        self.c_arc = 152
        self.c_aic = 184
        self.c_a128 = 216
        self.c_car = 280
        self.c_q = 344
        self.c_t = 352
        self.c_lam = 368
        self.c_dtc = 464

    def mm(self, out, lhsT, rhs, start, stop, skip=False):
        rd = [lhsT, rhs] + ([] if start else [out])
        if skip:
            self.P.add("pe", lambda e: e.matmul(out, lhsT, rhs, start=start, stop=stop, skip_group_check=True), rd, [out])
        else:
            self.P.add("pe", lambda e: e.matmul(out, lhsT, rhs, start=start, stop=stop), rd, [out])

    def tp(self, out, in_):
        ident = self.ident[:]
        self.P.add("pe", lambda e: e.transpose(out, in_, ident), [in_, ident], [out])

    def act(self, out, in_, func, scale=1.0, bias=None, eng="act", accum=None):
        rd = [in_]
        kw = {}
        if accum is not None:
            kw["accum_out"] = accum
        if not isinstance(scale, (int, float)):
            rd.append(scale)
        kw["scale"] = scale
        if bias is not None:
            kw["bias"] = bias
            if not isinstance(bias, (int, float)):
                rd.append(bias)
        self.P.add("act", lambda e: e.activation(out, in_, func, **kw), rd, [out] + ([accum] if accum is not None else []), force_sync=(accum is not None))

    def tt(self, out, a, b, op, eng="dve"):
        self.P.add(eng, lambda e: e.tensor_tensor(out, a, b, op), [a, b], [out])

    def ts(self, out, in_, s1, s2, op0, op1=None, eng="dve"):
        rd = [in_]
        for s in (s1, s2):
            if s is not None and not isinstance(s, (int, float)):
                rd.append(s)
        if op1 is None:
            self.P.add(eng, lambda e: e.tensor_scalar(out, in_, s1, None, op0), rd, [out])
        else:
            self.P.add(eng, lambda e: e.tensor_scalar(out, in_, s1, s2, op0, op1), rd, [out])

    def stt(self, out, in0, scalar, in1, op0, op1):
        rd = [in0, in1]
        if not isinstance(scalar, (int, float)):
            rd.append(scalar)
        self.P.add("dve", lambda e: e.scalar_tensor_tensor(out, in0, scalar, in1, op0, op1), rd, [out])

    def cp(self, out, in_, eng="dve"):
        self.P.add(eng, lambda e: e.tensor_copy(out, in_), [in_], [out])

    def dma(self, out, in_, slot, q="sp"):
        rd = [] if str(in_.space) == "DRAM" and in_.tensor.name not in ("s5c", "x1s") else [in_]
        self.P.add(q, lambda e: e.dma_start(out=out, in_=in_), rd, [out], dma=slot)

    def col(self, c, n=1):
        return self.sm[:, c:c + n]

    def slot(self):
        s = self.ps2[self.slot_i % 4]
        self.slot_i += 1
        return s

    def loadw(self, src):
        i = self.wcount
        self.wcount += 1
        st = self.wst[i % self.NS]
        bf = self.wbf[i % self.NB]
        self.dma(st[:], src, f"w{i % self.NS}")
        ce = self.cast_engs[i % len(self.cast_engs)]
        flat = bf[:].rearrange("p a b -> p (a b)")
        if ce == "act":
            self.act(flat, st[:], AF.Copy)
        else:
            self.cp(flat, st[:], eng=ce)
        return bf

    def wstream_begin(self, src, n, look=3):
        self.ws_src = src
        self.ws_n = n
        self.ws_next = 0
        self.ws_fifo = []
        for _ in range(look):
            self._ws_push()

    def _ws_push(self):
        if self.ws_next < self.ws_n:
            self.ws_fifo.append(self.loadw(self.ws_src[self.ws_next]))
            self.ws_next += 1

    def nextw(self):
        wb = self.ws_fifo.pop(0)
        self._ws_push()
        self.wi += 1
        return wb

    def rowstats(self, x, n, mean_col, rstd_col, rms, junk):
        ssq = self.col(self.c_mv)
        ms = self.col(self.c_ms)
        self.act(junk, x, AF.Square, accum=ssq)
        if rms:
            self.act(ms, ssq, AF.Sqrt, scale=1.0 / n, bias=EPS)
        else:
            sm_ = self.col(self.c_mv + 1)
            self.act(junk, x, AF.Identity, accum=sm_)
            self.ts(mean_col, sm_, 1.0 / n, None, ALU.mult)
            t = self.col(self.c_mv + 2)
            self.tt(t, mean_col, mean_col, ALU.mult)
            self.stt(t, ssq, 1.0 / n, t, ALU.mult, ALU.subtract)
            self.act(ms, t, AF.Sqrt, bias=EPS)
        self.P.add("dve", lambda e: e.reciprocal(rstd_col, ms), [ms], [rstd_col])

    def gelu(self, out, in_, tmp):
        if self.gelu_native:
            self.act(out, in_, AF.Gelu_apprx_tanh)
            return
        self.act(tmp, in_, AF.Square)
        self.ts(tmp, tmp, 0.044715, 1.0, ALU.mult, ALU.add)
        self.tt(tmp, tmp, in_, ALU.mult)
        self.act(tmp, tmp, AF.Sigmoid, scale=GELU_C)
        self.tt(out, tmp, in_, ALU.mult)

    def views(self):
        Bb = self.BIGb
        self.uT = Bb[:, 0:8192].rearrange("p (a b) -> p a b", b=512)
        self.vtok = Bb[:, 8192:16384].rearrange("p (a b) -> p a b", b=2048)
        self.xbT = Bb[:, 16384:20480].rearrange("p (a b) -> p a b", b=512)
        self.zT = Bb[:, 20480:24576].rearrange("p (a b) -> p a b", b=512)
        self.s5cb = [Bb[:, 24576:28672], Bb[:, 28672:32768]]
        Bf = self.BIG
        self.tmpF = Bf[:, 12288:14336]
        self.tmpF2 = Bf[:, 14336:16384]
        self.tmpZ = Bf[:, 10240:12288]
        self.vreg = Bf[:, 4096:8192]
        self.mix = Bf[:, :].rearrange("p (a b) -> p a b", b=4096)

    def prologue(self):
        P = self.P
        nc = self.nc
        sm = self.sm
        Bf = self.BIG
        onesf = Bf[:, 0:128]
        idf = Bf[:, 128:256]
        trf = Bf[:, 256:384]
        P.add("pool", lambda e: e.memset(onesf, 1.0), [], [onesf])
        P.add("pool", lambda e: e.affine_select(idf, onesf, [[-1, 128]], ALU.is_equal, 0.0, base=0, channel_multiplier=1), [onesf], [idf])
        P.add("pool", lambda e: e.affine_select(trf, onesf, [[1, 128]], ALU.is_ge, 0.0, base=0, channel_multiplier=-1), [onesf], [trf])
        self.cp(self.ident[:], idf)
        self.cp(self.tri[:], trf)
        scol = self.col(self.c_scol)
        P.add("pool", lambda e: e.iota(scol, [[0, 1]], base=0, channel_multiplier=1, allow_small_or_imprecise_dtypes=True), [], [scol])
        trow = self.trow[:]
        P.add("pool", lambda e: e.iota(trow, [[1, 128]], base=0, channel_multiplier=0, allow_small_or_imprecise_dtypes=True), [], [trow])
        self.ts(self.col(self.c_nscol), scol, -1.0, None, ALU.mult)
        self.dma(sm[:, self.c_gcol:self.c_gcol + 64], self.gcol_d[:, :], "misc")
        self.dma(sm[:, self.c_dcol:self.c_dcol + 8], self.dcol_d[:, :], "misc")
        self.dma(sm[:, self.c_lam:self.c_lam + 96], self.lamcol_d[:, :], "misc")
        wsf = Bf[:, 2048:4096]
        self.dma(wsf, self.wsT_d[:, :], "misc")
        wsf3 = wsf.rearrange("p (h t) -> p h t", t=128)
        P.add("pool", lambda e: e.affine_select(wsf3, wsf3, [[0, 16], [1, 128]], ALU.is_ge, 0.0, base=0, channel_multiplier=-1), [wsf3], [wsf3])
        self.cp(self.wsT[:], wsf3)
        car = sm[:, self.c_car:self.c_car + 64]
        P.add("dve", lambda e: e.memset(car, 0.0), [], [car])
        lre = sm[:, self.c_lam:self.c_lam + 32]
        lim = sm[:, self.c_lam + 32:self.c_lam + 64]
        ldt = sm[:, self.c_lam + 64:self.c_lam + 96]
        dtc = sm[:, self.c_dtc:self.c_dtc + 32]
        arc = sm[:, self.c_arc:self.c_arc + 32]
        aic = sm[:, self.c_aic:self.c_aic + 32]
        self.act(dtc, ldt, AF.Exp)
        self.tt(arc, lre, dtc, ALU.mult)
        self.tt(aic, lim, dtc, ALU.mult)
        t0 = sm[:, self.c_t:self.c_t + 32] if False else Bf[:, 384:416]
        t1 = Bf[:, 416:448]
        t2 = Bf[:, 448:480]
        a128re = sm[:, self.c_a128:self.c_a128 + 32]
        a128im = sm[:, self.c_a128 + 32:self.c_a128 + 64]
        self.act(t0, arc, AF.Exp, scale=128.0)
        self.ts(t1, aic, 128.0, None, ALU.mult)
        self.sincos(t1, t2, a128im, a128re, t0)
        if self.NPRE > 0:
            self.prefix_s0_first()
        for k in range(8):
            ks = slice(k * 512, (k + 1) * 512)
            W = Bf[:, 4096:16384]
            lrb = W[:, 0:512]
            lib = W[:, 512:1024]
            ldb = W[:, 1024:1536]
            bre = W[:, 1536:2048]
            bim = W[:, 2048:2560]
            cre_ = W[:, 2560:3584]
            dtb = W[:, 3584:4096]
            arb = W[:, 4096:4608]
            aib = W[:, 4608:5120]
            mag = W[:, 5120:5632]
            sn = W[:, 5632:6144]
            cs = W[:, 6144:6656]
            w0 = W[:, 6656:7168]
            w1 = W[:, 7168:7680]
            w2 = W[:, 7680:8192]
            w3 = W[:, 8192:8704]
            cb = self.xbuf[:, :]
            self.dma(lrb, self.lam_bc[0, :, ks], "pa")
            self.dma(lib, self.lam_bc[1, :, ks], "pb")
            self.dma(ldb, self.lam_bc[2, :, ks], "pc")
            self.dma(W[:, 1536:2560], self.bbd[k, :, :], "pd")
            self.dma(cre_, self.cbd[k, :, :], "pe_")
            self.act(dtb, ldb, AF.Exp)
            self.tt(arb, lrb, dtb, ALU.mult)
            self.tt(aib, lib, dtb, ALU.mult)
            self.act(mag, arb, AF.Exp)
            self.sincos(aib, w2, w0, w1, mag)
            self.ts(w1, w1, -1.0, None, ALU.add)
            self.tt(w2, lrb, lrb, ALU.mult)
            self.tt(w3, lib, lib, ALU.mult)
            self.tt(w2, w2, w3, ALU.add)
            P.add("dve", lambda e, w2=w2: e.reciprocal(w2, w2), [w2], [w2])
            self.tt(sn, w1, lrb, ALU.mult)
            self.tt(w3, w0, lib, ALU.mult)
            self.tt(sn, sn, w3, ALU.add)
            self.tt(sn, sn, w2, ALU.mult)
            self.tt(cs, w0, lrb, ALU.mult)
            self.tt(w3, w1, lib, ALU.mult)
            self.tt(cs, cs, w3, ALU.subtract)
            self.tt(cs, cs, w2, ALU.mult)
            self.tt(w0, bre, sn, ALU.mult)
            self.tt(w1, bim, cs, ALU.mult)
            self.tt(cb[:, 0:512], w0, w1, ALU.subtract)
            self.tt(w0, bre, cs, ALU.mult)
            self.tt(w1, bim, sn, ALU.mult)
            self.tt(cb[:, 512:1024], w0, w1, ALU.add)
            self.act(mag, arb, AF.Exp, scale=self.col(self.c_nscol))
            self.ts(w2, aib, self.col(self.c_scol), None, ALU.mult)
            self.sincos(w2, w3, w0, w1, mag)
            self.cp(cb[:, 1024:1536], w1)
            self.ts(cb[:, 1536:2048], w0, -1.0, None, ALU.mult)
            for i in range(4):
                j = 4 * k + i
                am = W[:, 8704:8832]
                an = W[:, 8832:8960]
                a0 = W[:, 8960:9088]
                a1 = W[:, 9088:9216]
                a2 = W[:, 9216:9344]
                self.act(am, self.trow[:], AF.Exp, scale=self.col(self.c_arc + j))
                self.ts(an, self.trow[:], self.col(self.c_aic + j), None, ALU.mult)
                self.sincos(an, a2, a0, a1, am)
                self.cp(cb[:, 2048 + i * 128:2048 + (i + 1) * 128], a1)
                self.cp(cb[:, 2560 + i * 128:2560 + (i + 1) * 128], a0)
            self.cp(cb[:, 3072:3584], cre_[:, 0:512])
            self.ts(cb[:, 3584:4096], cre_[:, 512:1024], -1.0, None, ALU.mult)
            self.dma(self.s5c[k, :, :], cb, "s5w")

    def sincos(self, ang, tmp, out_sin, out_cos, mag):
        ti = tmp.bitcast(mybir.dt.int32)
        self.ts(tmp, ang, 1.0 / (2 * PI), None, ALU.mult)
        self.cp(ti, tmp)
        self.cp(tmp, ti)
        self.stt(tmp, tmp, -2 * PI, ang, ALU.mult, ALU.add)
        self.ts(out_cos, tmp, PI, 2 * PI, ALU.is_gt, ALU.mult)
        self.tt(tmp, tmp, out_cos, ALU.subtract)
        self.ts(out_cos, tmp, -PI, 2 * PI, ALU.is_lt, ALU.mult)
        self.tt(tmp, tmp, out_cos, ALU.add)
        self.ts(tmp, tmp, PI, -PI, ALU.min, ALU.max)
        self.act(out_sin, tmp, AF.Sin)
        self.tt(out_sin, out_sin, mag, ALU.mult)
        self.stt(out_cos, tmp, -1.0, tmp, ALU.mult, ALU.max)
        self.ts(out_cos, out_cos, -1.0, PI / 2, ALU.mult, ALU.add)
        self.act(out_cos, out_cos, AF.Sin)
        self.tt(out_cos, out_cos, mag, ALU.mult)

    def s0_from(self, get_rows, gidx):
        hT = self.actA
        for tb in range(4):
            x = get_rows(tb)
            rstd = self.col(self.c_stat + tb)
            if self.stop == 's0x':
                continue
            self.rowstats(x, D, None, rstd, True, self.xbuf[:, :])
            if self.stop == 's0a':
                continue
            self.ts(self.xbuf[:, :], x, rstd, None, ALU.mult)
            if self.stop == 's0b':
                continue
            self.transposes(tb, gidx)

    def transposes(self, tb, gidx, fixed=None):
        hT = self.actA
        for g8 in range(4):
            psb = (fixed if fixed is not None else self.slot())[:, :].rearrange("p (a b) -> p a b", b=128)
            for j in range(8):
                kt = g8 * 8 + j
                self.mm(psb[:, j, :], self.xbuf[:, kt * 128:(kt + 1) * 128], self.ident[:, :], True, True)
            if self.stop == 's0c':
                continue
            for j in range(8):
                kt = g8 * 8 + j
                if self.stop == 's0d':
                    self.cp(hT[:, kt, tb * 128:(tb + 1) * 128], psb[:, j, :])
                    continue
                if j % 2 == 0:
                    self.act(hT[:, kt, tb * 128:(tb + 1) * 128], psb[:, j, :], AF.Copy,
                             scale=self.col(self.c_gcol + gidx * 32 + kt))
                else:
                    self.ts(hT[:, kt, tb * 128:(tb + 1) * 128], psb[:, j, :],
                            self.col(self.c_gcol + gidx * 32 + kt), None, ALU.mult)

    def s0_tb_alt(self, xsrc, row0, tb, gidx, upper_only=False, use_slot=False):
        stg = self.actB.bitcast(F32)[:].rearrange("p a b -> p (a b)")
        half = 1 if upper_only else tb % 2
        x = stg[:, half * 4096:(half + 1) * 4096]
        self.dma(x, xsrc[row0 + tb * 128:row0 + (tb + 1) * 128, :], f"xa{half}")
        rstd = self.col(self.c_stat + tb)
        self.rowstats(x, D, None, rstd, True, self.xbuf[:, :])
        self.ts(self.xbuf[:, :], x, rstd, None, ALU.mult)
        self.transposes(tb, gidx, fixed=None if use_slot else self.ps2[3])

    def s0(self, xsrc, row0, gidx):
        for tb in range(4):
            self.dma(self.mix[:, tb, :], xsrc[row0 + tb * 128:row0 + (tb + 1) * 128, :], f"x{tb}")
        self.s0_from(lambda tb: self.mix[:, tb, :], gidx)

    def fm_group(self, wbs, acts, ps):
        n = len(wbs)
        for bi, (wb, af) in enumerate(zip(wbs, acts)):
            for kt in range(16):
                for sub in range(2):
                    self.mm(ps[:, sub, :], wb[:, kt, sub * 128:(sub + 1) * 128], af(kt),
                            start=(bi == 0 and kt == 0), stop=(bi == n - 1 and kt == 15))

    def fm(self, srcs, actfn_list):
        ps = self.slot()[:, :].rearrange("p (a b) -> p a b", b=512)
        n = len(srcs)
        for bi in range(n):
            wb = srcs[bi]()
            af = actfn_list[bi]
            for kt in range(KB):
                for sub in range(2):
                    self.mm(ps[:, sub, :], wb[:, kt, sub * 128:(sub + 1) * 128], af(kt),
                            start=(bi == 0 and kt == 0), stop=(bi == n - 1 and kt == KB - 1))
        return ps

    def tm(self, srcs, actfn_list):
        ps = self.slot()[:, :].rearrange("p (a b) -> p a b", b=256)
        n = len(srcs)
        for bi in range(n):
            wb = srcs[bi]()
            af = actfn_list[bi]
            for kt in range(KB):
                for tb in range(4):
                    self.mm(ps[:, tb, :], af(kt, tb), wb[:, kt, :],
                            start=(bi == 0 and kt == 0 and tb % 2 == 0), stop=(bi == n - 1 and kt == KB - 1), skip=True)
        return ps

    @staticmethod
    def kf(T, n):
        return [lambda kt, b=b: T[:, KB * b + kt, :] for b in range(n)]

    @staticmethod
    def kft(T, n):
        return [lambda kt, tb, b=b: T[:, KB * b + kt, tb * 128:(tb + 1) * 128] for b in range(n)]

    def s5_load_consts(self, k):
        cb = self.s5cb[k % 2]
        self.dma(cb, self.s5c[k, :, :], f"s5c{k % 2}")
        return cb

    def s5_x(self, cb, k, c, Xps):
        lhsT = self.xbT[:, k, c * 128:(c + 1) * 128]
        self.mm(Xps[:, 0:512], lhsT, cb[:, 0:512], True, True)
        self.mm(Xps[:, 512:1024], lhsT, cb[:, 512:1024], True, True)

    def s5_pre(self, cb, Xps, Xp=None):
        vr = self.vreg
        if Xp is None:
            Xp = vr[:, 1024:1536].bitcast(BF16)
        t1 = vr[:, 0:1024]
        t2 = self.t2_ap if getattr(self, 't2_ap', None) is not None else self.xbufF[:, 1024:2048]
        t1v = t1.rearrange("p (a b) -> p a b", b=512)
        t2v = t2.rearrange("p (a b) -> p a b", b=512)
        X3 = Xps[:, :].rearrange("p (a b) -> p a b", b=512)
        cbt = cb.tensor
        rowlen = 1
        for d_ in list(cbt.shape)[1:]:
            rowlen *= d_
        Wre_b = bass.AP(cbt, cb.offset + 1024, [[rowlen, 128], [0, 2], [1, 512]])
        Wim_b = bass.AP(cbt, cb.offset + 1536, [[rowlen, 128], [0, 2], [1, 512]])
        self.tt(t1v, X3, Wre_b, ALU.mult)
        self.tt(t2v, X3, Wim_b, ALU.mult)
        self.tt(Xp[:, 0:512], t1[:, 0:512], t2[:, 512:1024], ALU.subtract)
        self.tt(Xp[:, 512:1024], t2[:, 0:512], t1[:, 512:1024], ALU.add)
        return Xp

    def s5_carry_update(self, k, p127, has_carry=False):
        sm = self.sm
        car = sm[:, self.c_car + 8 * k:self.c_car + 8 * k + 8]
        q = sm[:, self.c_q:self.c_q + 8]
        t = sm[:, self.c_t:self.c_t + 16]
        are = sm[:, self.c_a128 + 4 * k:self.c_a128 + 4 * k + 4]
        aim = sm[:, self.c_a128 + 32 + 4 * k:self.c_a128 + 32 + 4 * k + 4]
        if has_carry:
            q = p127
        else:
            self.tt(q, p127, car, ALU.add)
        self.tt(t[:, 0:4], q[:, 0:4], are, ALU.mult)
        self.tt(t[:, 4:8], q[:, 4:8], aim, ALU.mult)
        self.tt(t[:, 8:12], q[:, 0:4], aim, ALU.mult)
        self.tt(t[:, 12:16], q[:, 4:8], are, ALU.mult)
        self.tt(car[:, 0:4], t[:, 0:4], t[:, 4:8], ALU.subtract)
        self.tt(car[:, 4:8], t[:, 8:12], t[:, 12:16], ALU.add)

    def s5_state_only(self, next_s0=None):
        sm = self.sm
        self.t2_ap = self.vreg[:, 2048:3072]
        Bb = self.BIGb
        fcs = [Bb[:, 0:8192].rearrange("p (k c) -> p k c", c=2048),
               Bb[:, 20480:28672].rearrange("p (k c) -> p k c", c=2048)]
        for h in range(2):
            self.dma(fcs[h], self.s5c[4 * h:4 * h + 4, :, 0:2048].rearrange("k p c -> p k c"), f"s5f{h}")
        vr = self.vreg
        car = sm[:, self.c_car:self.c_car + 64]
        car3 = car.rearrange("p (k c) -> p k c", c=8)
        are = sm[:, self.c_a128:self.c_a128 + 32].rearrange("p (k c) -> p k c", c=4)
        aim = sm[:, self.c_a128 + 32:self.c_a128 + 64].rearrange("p (k c) -> p k c", c=4)
        P1 = [vr[:, 0:512].bitcast(BF16), vr[:, 512:1024].bitcast(BF16)]
        P2 = [vr[:, 1024:1536].bitcast(BF16), vr[:, 1536:2048].bitcast(BF16)]
        q = vr[:, 2048:2112]
        q3 = q.rearrange("p (k c) -> p k c", c=8)
        T = [vr[:, 2112 + 32 * i:2144 + 32 * i].rearrange("p (k c) -> p k c", c=4) for i in range(4)]
        csb = vr[:, 2304:2432]
        A3 = csb[:, 0:64].rearrange("p (k c) -> p k c", c=8)
        B3 = csb[:, 64:128].rearrange("p (k c) -> p k c", c=8)
        xbank = [self.ps2[0], self.ps2[2]]
        steps = [(c, k) for c in range(4) for k in range(8)]
        self.s5_x(fcs[0][:, 0, :], 0, 0, xbank[0])
        for c in range(4):
            cs = self.ps2[1][:, 0:128]
            for k in range(8):
                si = c * 8 + k
                cb = fcs[k // 4][:, k % 4, :]
                Xps = xbank[si % 2]
                if si + 1 < 32:
                    c2, k2 = steps[si + 1]
                    self.s5_x(fcs[k2 // 4][:, k2 % 4, :], k2, c2, xbank[(si + 1) % 2])
                p1 = P1[si % 2]
                p2 = P2[si % 2]
                X3 = Xps[:, :].rearrange("p (a b) -> p a b", b=512)
                cbt = cb.tensor
                rowlen = 1
                for d_ in list(cbt.shape)[1:]:
                    rowlen *= d_
                Wre_b = bass.AP(cbt, cb.offset + 1024, [[rowlen, 128], [0, 2], [1, 512]])
                Wim_b = bass.AP(cbt, cb.offset + 1536, [[rowlen, 128], [0, 2], [1, 512]])
                self.tt(p1.rearrange("p (a b) -> p a b", b=512), X3, Wre_b, ALU.mult)
                self.tt(p2.rearrange("p (a b) -> p a b", b=512), X3, Wim_b, ALU.mult)
                for i in range(8):
                    self.mm(cs[:, 8 * k + i:8 * k + i + 1], p1[:, i * 128:(i + 1) * 128], self.tri[:, 127:128], True, True)
                for i in range(8):
                    self.mm(cs[:, 64 + 8 * k + i:64 + 8 * k + i + 1], p2[:, i * 128:(i + 1) * 128], self.tri[:, 127:128], True, True)
            self.cp(csb, cs)
            self.tt(q3[:, :, 0:4], A3[:, :, 0:4], B3[:, :, 4:8], ALU.subtract)
            self.tt(q3[:, :, 4:8], B3[:, :, 0:4], A3[:, :, 4:8], ALU.add)
            self.tt(q, q, car, ALU.add)
            self.tt(T[0], q3[:, :, 0:4], are, ALU.mult)
            self.tt(T[1], q3[:, :, 4:8], aim, ALU.mult)
            self.tt(T[2], q3[:, :, 0:4], aim, ALU.mult)
            self.tt(T[3], q3[:, :, 4:8], are, ALU.mult)
            self.tt(car3[:, :, 0:4], T[0], T[1], ALU.subtract)
            self.tt(car3[:, :, 4:8], T[2], T[3], ALU.add)
            if next_s0 is not None:
                next_s0(c)
        self.t2_ap = None

    def s5_full(self):
        sm = self.sm
        vr = self.vreg
        xbank = [self.ps2[0], self.ps2[3]]
        Xpb = [vr[:, 1024:1536].bitcast(BF16), self.gbc[:, 0:512].bitcast(BF16)]
        cbs = {0: self.s5_load_consts(0)}
        self.s5_x(cbs[0], 0, 0, xbank[0])
        self.s5_pre(cbs[0], xbank[0], Xpb[0])
        for k in range(8):
            cb = cbs[k]
            if k + 1 < 8:
                cbs[k + 1] = self.s5_load_consts(k + 1)
            yps = self.ps2[2][:, (k % 2) * 512:(k % 2) * 512 + 512]
            Cm = cb[:, 3072:4096].rearrange("p (a b) -> p a b", b=128)
            for c in range(4):
                si = k * 4 + c
                Xp = Xpb[si % 2]
                if si + 1 < 32:
                    k2, c2 = (si + 1) // 4, (si + 1) % 4
                    self.s5_x(cbs[k2], k2, c2, xbank[(si + 1) % 2])
                PT = self.ps2[1][:, :].rearrange("p (a b) -> p a b", b=128)
                for i in range(8):
                    self.mm(PT[:, i, :], Xp[:, i * 128:(i + 1) * 128], self.tri[:, :], True, True)
                if si + 1 < 32:
                    self.s5_pre(cbs[k2], xbank[(si + 1) % 2], Xpb[(si + 1) % 2])
                Ptot = vr[:, 1536:2560]
                t1 = vr[:, 2560:3584]
                t2 = self.xbufF[:, 0:1024]
                sT = vr[:, 3584:4096].bitcast(BF16)
                for i in range(8):
                    ccol = sm[:, self.c_car + 8 * k + i:self.c_car + 8 * k + i + 1]
                    self.act(Ptot[:, i * 128:(i + 1) * 128], PT[:, i, :], AF.Identity, bias=ccol)
                cbt = cb.tensor
                rowlen = 1
                for d_ in list(cbt.shape)[1:]:
                    rowlen *= d_
                Are_b = bass.AP(cbt, cb.offset + 2048, [[rowlen, 128], [0, 2], [1, 512]])
                Aim_b = bass.AP(cbt, cb.offset + 2560, [[rowlen, 128], [0, 2], [1, 512]])
                P3 = Ptot.rearrange("p (a b) -> p a b", b=512)
                self.tt(t1.rearrange("p (a b) -> p a b", b=512), P3, Are_b, ALU.mult)
                self.tt(t2.rearrange("p (a b) -> p a b", b=512), P3, Aim_b, ALU.mult)
                self.tt(sT[:, 0:512], t1[:, 0:512], t2[:, 512:1024], ALU.subtract)
                self.tt(sT[:, 512:1024], t2[:, 0:512], t1[:, 512:1024], ALU.add)
                q127 = bass.AP(Ptot.tensor, Ptot.offset + 127, [[Ptot.ap[0][0], 128], [128, 8]])
                self.s5_carry_update(k, q127, has_carry=True)
                for i in range(8):
                    self.mm(yps[:, c * 128:(c + 1) * 128], Cm[:, i, :], sT[:, i * 128:(i + 1) * 128],
                            start=(i == 0), stop=(i == 7))
            ty = self.xbufF[:, 0:512]
            self.stt(ty, self.xbT[:, k, :], self.col(self.c_dcol + k), yps, ALU.mult, ALU.add)
            self.gelu(self.zT[:, k, :], ty, self.xbufF[:, 512:1024])

    def xb_proj(self, srcfn):
        hT = self.actA
        for g in range(4):
            ps = self.fm([srcfn] * 4, self.kf(hT, 4))
            self.act(self.xbT[:, 2 * g:2 * g + 2, :], ps, AF.Copy)

    def prefix_s0_first(self):
        for tb in range(4):
            self.s0_tb_alt(self.xp, 0, tb, 0)

    def prefix_tile(self, t):
        self.wi = 0
        if t == 0:
            self.wstream_begin(self.wxb, 16)
        self.xb_proj(self.nextw)
        nxt = None
        if t + 1 < self.NPRE:
            self.wstream_begin(self.wxb, 16)
            nxt = lambda c, t=t: self.s0_tb_alt(self.xp, (t + 1) * TT, c, 0)
        elif self.NT > 0:
            nxt = lambda c: self.s0_tb_alt(self.xo, 0, c, 0)
            self.s0_done = True
        self.s5_state_only(nxt)

    def main_tile(self, t):
        P = self.P
        hT = self.actA
        mT = self.actB
        row0 = t * TT
        self.wi = 0
        nw = self.nextw
        self.wstream_begin(self.wall, NBLK)
        if getattr(self, "s0_done", False):
            self.s0_done = False
        else:
            self.s0(self.xo, row0, 0)
        if self.stop in ('s0', 's0a', 's0b', 's0x', 's0c', 's0d'):
            return
        hfa = lambda kt: hT[:, kt, :]
        hfb = lambda kt: hT[:, 16 + kt, :]
        for g in range(8):
            ps = self.fm([nw] * 4, self.kf(hT, 4))
            self.gelu(self.uT[:, 2 * g:2 * g + 2, :], ps, self.xbufF[:, 0:1024].rearrange("p (a b) -> p a b", b=512))
        for g in range(8):
            ps = self.tm([nw] * 4, self.kft(hT, 4))
            self.gelu(self.vtok[:, :, g * 256:(g + 1) * 256], ps,
                      self.xbufF[:, 0:1024].rearrange("p (a b) -> p a b", b=256))
        self.xb_proj(nw)
        if self.stop == 's1':
            return
        self.dma(self.gbc[:, :], self.lngb[0, :, :], "gbc")
        self.dma(self.xbufF[:, :], self.lngb[1, :, :], "xbuf")
        for tb in range(4):
            mean = self.col(self.c_stat + 4 + tb)
            rstd = self.col(self.c_stat + 8 + tb)
            v = self.vtok[:, tb, :]
            self.rowstats(v, AW, mean, rstd, False, self.tmpF)
            self.stt(self.tmpF, v, mean, self.gbc[:, :], ALU.subtract, ALU.mult)
            self.stt(v, self.tmpF, rstd, self.xbufF[:, :], ALU.mult, ALU.add)
        self.dma(self.xbufF[:, :], self.bsp_d[:, :], "xbuf")
        for h in range(16):
            psA = self.ps2[1 + (h % 2)][:, 0:512]
            for tb in range(4):
                self.mm(psA[:, tb * 128:(tb + 1) * 128], self.vtok[:, tb, h * 128:(h + 1) * 128],
                        self.wsT[:, h, :], True, True)
            xbf = self.xbufF
            bias3 = bass.AP(xbf, h * 128, [[2048, 128], [0, 4], [1, 128]])
            ts_ = self.tmpZ[:, 0:512].rearrange("p (a b) -> p a b", b=128)
            self.tt(ts_, psA.rearrange("p (a b) -> p a b", b=128), bias3, ALU.add)
            self.tt(self.uT[:, h, :], self.tmpZ[:, 0:512], self.uT[:, h, :], ALU.mult)
        if self.stop == 's2':
            return
        self.s5_full()
        if self.stop == 's3':
            return
        vr = self.vreg
        tA = vr[:, 0:1024].rearrange("p (a b) -> p a b", b=512)
        tB = vr[:, 1024:2048].rearrange("p (a b) -> p a b", b=512)
        tG = vr[:, 2048:3072].rearrange("p (a b) -> p a b", b=512)
        sT_ = self.uT
        zT = self.zT
        for j in range(16):
            ps = self.fm([nw] * 2, self.kf(sT_, 2))
            self.act(tA, ps, AF.Copy)
            ps = self.fm([nw] * 4, self.kf(hT, 4))
            self.act(tG, ps, AF.Sigmoid)
            self.tt(tA, tA, tG, ALU.mult)
            psa = self.fm([nw], self.kf(zT, 1))
            psb = self.fm([nw], self.kf(zT, 1))
            self.act(tG, psb, AF.Sigmoid)
            self.tt(tB, psa, tG, ALU.mult)
            ps = self.fm([nw] * 4, self.kf(hT, 4))
            self.act(tG, ps, AF.Sigmoid)
            self.tt(tB, tB, tG, ALU.mult)
            self.tt(mT[:, 2 * j:2 * j + 2, :], tA, tB, ALU.add)
        if self.stop == 's4':
            return
        for j in range(16):
            ps = self.tm([nw] * 4, self.kft(mT, 4))
            self.act(self.mix[:, :, j * 256:(j + 1) * 256], ps, AF.Copy)
        for tb in range(4):
            self.rowstats(self.mix[:, tb, :], D, None, self.col(self.c_stat + tb), True, self.xbuf[:, :])
        for half in range(2):
            hs = slice(half * 2048, (half + 1) * 2048)
            self.dma(self.gbc[:, :], self.gains[0, :, hs], "gbc")
            for tb in range(4):
                self.dma(self.xbufF[:, :], self.xo[row0 + tb * 128:row0 + (tb + 1) * 128, hs], "xbuf")
                m = self.mix[:, tb, hs]
                self.stt(m, m, self.col(self.c_stat + tb), self.gbc[:, :], ALU.mult, ALU.mult)
                self.tt(m, m, self.xbufF[:, :], ALU.add)
        for tb in range(4):
            self.dma(self.x1s[row0 + tb * 128:row0 + (tb + 1) * 128, :], self.mix[:, tb, :], f"x1w{tb}")
        self.s0_from(lambda tb: self.mix[:, tb, :], 1)
        if self.stop == 's5':
            return
        aT = self.actB
        ffv = self.mix
        tR = self.xbufF[:, 0:1024].rearrange("p (a b) -> p a b", b=512)
        for c in range(8):
            for g in range(8):
                ps = self.fm([nw] * 4, self.kf(hT, 4))
                self.act(tR, ps, AF.Relu)
                self.tt(aT[:, 2 * g:2 * g + 2, :], tR, tR, ALU.mult)
            for j in range(16):
                ps = self.tm([nw] * 2, self.kft(aT, 2))
                dst = ffv[:, :, j * 256:(j + 1) * 256]
                if c == 0:
                    self.act(dst, ps, AF.Copy)
                else:
                    self.tt(dst, dst, ps, ALU.add)
                if c == 7 and t + 1 < self.NT and j % 4 == 3:
                    self.s0_tb_alt(self.xo, row0 + TT, j // 4, 0, upper_only=True, use_slot=True)
                    self.s0_done = True
        assert self.wi == NBLK, self.wi
        for tb in range(4):
            self.rowstats(ffv[:, tb, :], D, None, self.col(self.c_stat + tb), True, self.xbuf[:, :])
        for tb in range(4):
            for half in range(2):
                hs = slice(half * 2048, (half + 1) * 2048)
                self.dma(self.gbc[:, :], self.gains[1, :, hs], "gbc")
                self.dma(self.xbufF[:, :], self.x1s[row0 + tb * 128:row0 + (tb + 1) * 128, hs], "xbuf")
                m = ffv[:, tb, hs]
                self.stt(m, m, self.col(self.c_stat + tb), self.gbc[:, :], ALU.mult, ALU.mult)
                self.tt(m, m, self.xbufF[:, :], ALU.add)
            self.dma(self.out[row0 + tb * 128:row0 + (tb + 1) * 128, :], ffv[:, tb, :], f"ow{tb}")

    def build(self):
        self.views()
        self.prologue()
        if self.stop == 'pro':
            self.P.finalize()
            return self.nc
        for t in range(self.NPRE):
            self.prefix_tile(t)
        for t in range(self.NT):
            self.main_tile(t)
        if self.stop:
            self.dma(self.dbgA[:, :], self.actA[:].rearrange("p a b -> p (a b)"), "dbgA")
            self.dma(self.dbgB[:, :], self.actB[:].rearrange("p a b -> p (a b)"), "dbgB")
            self.dma(self.dbgBIG[:, :], self.BIG[:, :], "dbgBIG")
        self.P.finalize()
        return self.nc


def _blocks(W):
    K, N = W.shape
    a = W.reshape(K // 1024, 8, 128, N // 256, 256)
    a = a.transpose(3, 0, 2, 1, 4)
    return np.ascontiguousarray(a).reshape(N // 256, K // 1024, 128, 2048)


def _layout_weights(w_in, w_proj_a, w_glu_a, w_glu_b, w_out, w_ff_up, w_ff_down):
    blocks = []
    bu = _blocks(w_in[:, 0:2048])
    bv = _blocks(w_in[:, 2048:4096])
    bx = _blocks(w_in[:, 4096:5120])
    bga = _blocks(w_in[:, 5120:9216])
    bgb = _blocks(w_in[:, 9216:13312])
    for b in (bu, bv, bx):
        blocks.append(b.reshape(-1, 128, 2048))
    bpa = _blocks(w_proj_a)
    gla = _blocks(w_glu_a)
    glb = _blocks(w_glu_b)
    s4 = np.concatenate([bpa, bga, gla, glb, bgb], axis=1)
    blocks.append(s4.reshape(-1, 128, 2048))
    blocks.append(_blocks(w_out).reshape(-1, 128, 2048))
    for c in range(8):
        up = _blocks(w_ff_up[:, c * 2048:(c + 1) * 2048])
        dn = _blocks(w_ff_down[c * 2048:(c + 1) * 2048, :])
        blocks.append(up.reshape(-1, 128, 2048))
        blocks.append(dn.reshape(-1, 128, 2048))
    wall = np.concatenate(blocks, axis=0)
    assert wall.shape[0] == NBLK, wall.shape
    wxb = np.ascontiguousarray(bx.reshape(-1, 128, 2048))
    return wall, wxb


def _layout_small(norm_mix_pre, norm_mix_post, norm_mlp_pre, norm_mlp_post, v_norm_g, v_norm_b,
                  w_spatial, b_spatial, lam_re, lam_im, log_dt, b_re, b_im, c_re, c_im, d_skip):
    f = np.float32
    m = {}
    m["gains"] = np.ascontiguousarray(np.stack([np.broadcast_to(norm_mix_post[0], (128, D)),
                                                np.broadcast_to(norm_mlp_post[0], (128, D))]).astype(f))
    m["gcol"] = np.ascontiguousarray(np.concatenate([norm_mix_pre[0].reshape(32, 128).T,
                                                     norm_mlp_pre[0].reshape(32, 128).T], axis=1).astype(f))
    m["lngb"] = np.ascontiguousarray(np.stack([np.broadcast_to(v_norm_g[0], (128, AW)),
                                               np.broadcast_to(v_norm_b[0], (128, AW))]).astype(f))
    m["wsT"] = np.ascontiguousarray(w_spatial[0].transpose(2, 0, 1).reshape(128, 2048).astype(f))
    m["bsp"] = np.ascontiguousarray(np.broadcast_to(b_spatial[0].reshape(1, 2048), (128, 2048)).astype(f))
    ldt = np.repeat(log_dt[0][:, None], 64, axis=1)
    flat = np.stack([lam_re[0].reshape(-1), lam_im[0].reshape(-1), ldt.reshape(-1)])
    m["lam_bc"] = np.ascontiguousarray(np.broadcast_to(flat[:, None, :], (3, 128, 4096)).astype(f))
    m["lamcol"] = np.ascontiguousarray(np.concatenate([flat[i].reshape(32, 128).T for i in range(3)], axis=1).astype(f))
    bbd = np.zeros((8, 128, 1024), f)
    cbd = np.zeros((8, 128, 1024), f)
    for k in range(8):
        for gp in range(8):
            g = 8 * k + gp
            bbd[k, 16 * gp:16 * gp + 16, 64 * gp:64 * gp + 64] = b_re[0, g].T
            bbd[k, 16 * gp:16 * gp + 16, 512 + 64 * gp:512 + 64 * gp + 64] = b_im[0, g].T
            i, g2 = gp // 2, gp % 2
            cbd[k, g2 * 64:g2 * 64 + 64, i * 128 + 16 * gp:i * 128 + 16 * gp + 16] = c_re[0, g].T
            cbd[k, g2 * 64:g2 * 64 + 64, 512 + i * 128 + 16 * gp:512 + i * 128 + 16 * gp + 16] = c_im[0, g].T
    m["bbd"] = bbd
    m["cbd"] = cbd
    m["dcol"] = np.ascontiguousarray(d_skip[0].reshape(8, 128).T.astype(f))
    return m


_NC_CACHE = {}


def kernel(x, norm_mix_pre, w_in, v_norm_g, v_norm_b, w_spatial, b_spatial, w_proj_a,
           lam_re, lam_im, log_dt, b_re, b_im, c_re, c_im, d_skip, w_glu_a, w_glu_b,
           w_out, norm_mix_post, norm_mlp_pre, w_ff_up, w_ff_down, norm_mlp_post):
    x = np.asarray(x, np.float32)
    A = lambda a: np.asarray(a, np.float32)
    wall, wxb = _layout_weights(A(w_in)[0], A(w_proj_a)[0], A(w_glu_a)[0], A(w_glu_b)[0],
                                A(w_out)[0], A(w_ff_up)[0], A(w_ff_down)[0])
    small = _layout_small(A(norm_mix_pre), A(norm_mix_post), A(norm_mlp_pre), A(norm_mlp_post),
                          A(v_norm_g), A(v_norm_b), A(w_spatial), A(b_spatial), A(lam_re), A(lam_im),
                          A(log_dt), A(b_re), A(b_im), A(c_re), A(c_im), A(d_skip))
    NPRE = 12
    nc = Builder(NPRE=NPRE, NT=4).build()
    in_maps = []
    for c in range(NCORES):
        b, q = c // 4, c % 4
        xo = np.ascontiguousarray(x[b, q * OWN:(q + 1) * OWN])
        xp = np.zeros((NPRE * TT, D), np.float32)
        if q > 0:
            xp[NPRE * TT - q * OWN:] = x[b, 0:q * OWN]
        d = {"xo": xo, "xp": xp, "wall": wall, "wxb": wxb}
        d.update(small)
        in_maps.append(d)
    res = run_bass_kernel_spmd(nc, in_maps, core_ids=list(range(NCORES)))
    out = np.zeros((2, 8192, D), np.float32)
    for c in range(NCORES):
        b, q = c // 4, c % 4
        out[b, q * OWN:(q + 1) * OWN] = res.results[c]["out"]
    return out
```

```python
import numpy as np
import concourse.bass as bass
import concourse.mybir as mybir
from concourse.bass_utils import run_bass_kernel_spmd

F32 = mybir.dt.float32
BF16 = mybir.dt.bfloat16
AF = mybir.ActivationFunctionType
ALU = mybir.AluOpType
AX = mybir.AxisListType

D = 4096
AW = 2048
BW = 1024
DFF = 16384
TT = 512
OWN = 2048
NCORES = 8
EPS = 1e-6
NBLK = 848
KB = 8
PI = float(np.pi)
GELU_C = 1.5957691216057308


def _es(dt):
    return mybir.dt.size(dt)


class Ins:
    __slots__ = ("eng", "fn", "deps", "dma", "signal", "sigval", "dmaval", "small")


class Prog:
    PAGE = 512
    COMPUTE = ("pe", "act", "dve", "pool")

    def __init__(self, nc):
        self.nc = nc
        self.ins = []
        self.pages = {}
        self.dma_cnt = {}

    def _pages(self, ap):
        t = ap.tensor
        es = _es(ap.dtype)
        pat = ap.ap
        off = ap.offset
        sp = str(ap.space)
        if sp in ("SB", "PSUM"):
            row = _es(t.dtype)
            for s in list(t.shape)[1:]:
                row *= s
            lo = (off * es) % row
            ext = 1
            for st, cnt in pat[1:]:
                ext += (cnt - 1) * abs(st)
            page = 2048 if sp == "PSUM" else self.PAGE
        else:
            lo = off * es
            ext = 1
            for st, cnt in pat:
                ext += (cnt - 1) * abs(st)
            page = 65536
        hi = lo + ext * es
        key = (sp, t.name)
        return [(key, p) for p in range(lo // page, (hi - 1) // page + 1)]

    def add(self, eng, fn, reads=(), writes=(), dma=None, force_sync=False):
        idx = len(self.ins)
        small = force_sync
        for ap in list(writes) + list(reads):
            n = 1
            for st, cnt in ap.ap[1:]:
                n *= cnt
            if n <= 64:
                small = True
        deps = set()
        rp = []
        wp = []
        for ap in reads:
            if str(ap.space) == "PSUM":
                wp.extend(self._pages(ap))
            else:
                rp.extend(self._pages(ap))
        for ap in writes:
            wp.extend(self._pages(ap))
        for pg in rp:
            rec = self.pages.get(pg)
            if rec is not None and rec[0] is not None:
                deps.add(rec[0])
        for pg in wp:
            rec = self.pages.get(pg)
            if rec is not None:
                if rec[0] is not None:
                    deps.add(rec[0])
                deps.update(rec[1].values())
                deps.update(rec[2])
        for pg in rp:
            rec = self.pages.get(pg)
            if rec is None:
                rec = [None, {}, set()]
                self.pages[pg] = rec
            if dma is not None:
                rec[2].add(idx)
            else:
                rec[1][eng] = idx
        for pg in wp:
            self.pages[pg] = [idx, {}, set()]
        i = Ins()
        i.eng = eng
        i.fn = fn
        i.deps = deps
        i.dma = dma
        i.signal = False
        i.sigval = 0
        i.dmaval = 0
        i.small = small
        if dma is not None:
            self.dma_cnt[dma] = self.dma_cnt.get(dma, 0) + 1
            i.dmaval = 16 * self.dma_cnt[dma]
        self.ins.append(i)
        return idx

    def finalize(self, same_engine_sync=True):
        nc = self.nc
        ins = self.ins
        need = []
        for k, i in enumerate(ins):
            w = []
            for d in i.deps:
                di = ins[d]
                if di.dma is not None:
                    w.append(d)
                elif di.eng == i.eng and i.dma is None and (di.eng == "pe" or not same_engine_sync
                                                            or not (di.small or i.small)):
                    continue
                else:
                    di.signal = True
                    w.append(d)
            need.append(w)
        cnt = {e: 0 for e in ("pe", "act", "dve", "pool", "sp")}
        for i in ins:
            if i.dma is None and i.signal:
                cnt[i.eng] += 1
                i.sigval = cnt[i.eng]
        sems = {}
        for e in cnt:
            sems[e] = nc.alloc_semaphore(name=f"sem_{e}")
        dsems = {}
        for s in self.dma_cnt:
            dsems[s] = nc.alloc_semaphore(name=f"dsem_{s}")
        per_eng = {e: [] for e in cnt}
        for k, i in enumerate(ins):
            per_eng[i.eng].append(k)
        last_dma = {}
        prev_same_slot = {}
        for k, i in enumerate(ins):
            if i.dma is not None:
                if i.dma in last_dma:
                    prev_same_slot[k] = last_dma[i.dma]
                last_dma[i.dma] = k

        def emit(engname, engine):
            waited = {}
            for k in per_eng[engname]:
                i = ins[k]
                wl = {}
                for d in need[k]:
                    di = ins[d]
                    if di.dma is not None:
                        key = ("d", di.dma)
                        val = di.dmaval
                    else:
                        key = ("e", di.eng)
                        val = di.sigval
                    if wl.get(key, 0) < val:
                        wl[key] = val
                if k in prev_same_slot:
                    di = ins[prev_same_slot[k]]
                    key = ("d", di.dma)
                    if wl.get(key, 0) < di.dmaval:
                        wl[key] = di.dmaval
                for key, val in wl.items():
                    if waited.get(key, 0) >= val:
                        continue
                    waited[key] = val
                    sem = dsems[key[1]] if key[0] == "d" else sems[key[1]]
                    engine.wait_ge(sem, val)
                inst = i.fn(engine)
                if i.dma is not None:
                    inst.then_inc(dsems[i.dma], 16)
                elif i.signal:
                    inst.then_inc(sems[engname], 1)
            if engname == "sp":
                for s, c in self.dma_cnt.items():
                    if waited.get(("d", s), 0) < 16 * c:
                        engine.wait_ge(dsems[s], 16 * c)

        with nc.Block() as block:
            @block.tensor
            def _(e):
                emit("pe", e)

            @block.scalar
            def _(e):
                emit("act", e)

            @block.vector
            def _(e):
                emit("dve", e)

            @block.gpsimd
            def _(e):
                emit("pool", e)

            @block.sync
            def _(e):
                emit("sp", e)


class Builder:
    def __init__(self, NPRE=12, NT=4, gelu_native=True, NS=4, NB=4, cast_engs=("act", "dve"), stop=None):
        self.stop = stop
        self.NPRE = NPRE
        self.NT = NT
        self.gelu_native = gelu_native
        self.NS = NS
        self.NB = NB
        self.cast_engs = cast_engs
        nc = bass.Bass("TRN2", target_bir_lowering=False)
        self.nc = nc
        self.P = Prog(nc)
        dt = nc.dram_tensor
        self.xo = dt("xo", [NT * TT, D], F32, kind="ExternalInput")
        self.xp = dt("xp", [max(NPRE, 1) * TT, D], F32, kind="ExternalInput")
        self.wall = dt("wall", [NBLK, 128, 2048], F32, kind="ExternalInput")
        self.wxb = dt("wxb", [16, 128, 2048], F32, kind="ExternalInput")
        self.gains = dt("gains", [2, 128, D], F32, kind="ExternalInput")
        self.gcol_d = dt("gcol", [128, 64], F32, kind="ExternalInput")
        self.lngb = dt("lngb", [2, 128, AW], F32, kind="ExternalInput")
        self.wsT_d = dt("wsT", [128, 2048], F32, kind="ExternalInput")
        self.bsp_d = dt("bsp", [128, 2048], F32, kind="ExternalInput")
        self.lam_bc = dt("lam_bc", [3, 128, 4096], F32, kind="ExternalInput")
        self.lamcol_d = dt("lamcol", [128, 96], F32, kind="ExternalInput")
        self.bbd = dt("bbd", [8, 128, 1024], F32, kind="ExternalInput")
        self.cbd = dt("cbd", [8, 128, 1024], F32, kind="ExternalInput")
        self.dcol_d = dt("dcol", [128, 8], F32, kind="ExternalInput")
        self.out = dt("out", [NT * TT, D], F32, kind="ExternalOutput")
        if self.stop:
            self.dbgA = dt("dbgA", [128, 16384], BF16, kind="ExternalOutput")
            self.dbgB = dt("dbgB", [128, 16384], BF16, kind="ExternalOutput")
            self.dbgBIG = dt("dbgBIG", [128, 16384], F32, kind="ExternalOutput")
        self.s5c = dt("s5c", [8, 128, 4096], BF16, kind="Internal")
        self.x1s = dt("x1s", [NT * TT, D], F32, kind="Internal")

        sb = nc.alloc_sbuf_tensor
        self.wst = [sb(f"wst{i}", [128, 2048], F32) for i in range(NS)]
        self.wbf = [sb(f"wbf{i}", [128, KB, 256], BF16) for i in range(NB)]
        self.actA = sb("actA", [128, 32, 512], BF16)
        self.actB = sb("actB", [128, 32, 512], BF16)
        self.BIG = sb("BIG", [128, 16384], F32)
        self.BIGb = self.BIG.bitcast(BF16)
        self.gbc = sb("gbc", [128, 2048], F32)
        self.xbuf = sb("xbuf", [128, 4096], BF16)
        self.xbufF = self.xbuf.bitcast(F32)
        self.ident = sb("ident", [128, 128], BF16)
        self.tri = sb("tri", [128, 128], BF16)
        self.ntri = sb("ntri", [128, 128], BF16)
        self.wsT = sb("wsTb", [128, 16, 128], BF16)
        self.trow = sb("trow", [128, 128], F32)
        self.sm = sb("sm", [128, 512], F32)
        ps = nc.alloc_psum_tensor
        self.ps2 = [ps(f"ps{i}", [128, 1024], F32) for i in range(4)]
        self.slot_i = 0
        self.wcount = 0
        self.c_scol = 0
        self.c_nscol = 1
        self.c_stat = 2
        self.c_bn = 20
        self.c_mv = 70
        self.c_ms = 72
        self.c_gcol = 80
        self.c_dcol = 144
        self.c_arc = 152
        self.c_aic = 184
        self.c_a128 = 216
        self.c_car = 280
        self.c_q = 344
        self.c_t = 352
        self.c_lam = 368
        self.c_dtc = 464

    def mm(self, out, lhsT, rhs, start, stop, skip=False):
        rd = [lhsT, rhs] + ([] if start else [out])
        if skip:
            self.P.add("pe", lambda e: e.matmul(out, lhsT, rhs, start=start, stop=stop, skip_group_check=True), rd, [out])
        else:
            self.P.add("pe", lambda e: e.matmul(out, lhsT, rhs, start=start, stop=stop), rd, [out])

    def tp(self, out, in_):
        ident = self.ident[:]
        self.P.add("pe", lambda e: e.transpose(out, in_, ident), [in_, ident], [out])

    def act(self, out, in_, func, scale=1.0, bias=None, eng="act", accum=None):
        rd = [in_]
        kw = {}
        if accum is not None:
            kw["accum_out"] = accum
        if not isinstance(scale, (int, float)):
            rd.append(scale)
        kw["scale"] = scale
        if bias is not None:
            kw["bias"] = bias
            if not isinstance(bias, (int, float)):
                rd.append(bias)
        self.P.add("act", lambda e: e.activation(out, in_, func, **kw), rd, [out] + ([accum] if accum is not None else []), force_sync=(accum is not None))

    def tt(self, out, a, b, op, eng="dve"):
        self.P.add(eng, lambda e: e.tensor_tensor(out, a, b, op), [a, b], [out])

    def ts(self, out, in_, s1, s2, op0, op1=None, eng="dve"):
        rd = [in_]
        for s in (s1, s2):
            if s is not None and not isinstance(s, (int, float)):
                rd.append(s)
        if op1 is None:
            self.P.add(eng, lambda e: e.tensor_scalar(out, in_, s1, None, op0), rd, [out])
        else:
            self.P.add(eng, lambda e: e.tensor_scalar(out, in_, s1, s2, op0, op1), rd, [out])

    def stt(self, out, in0, scalar, in1, op0, op1):
        rd = [in0, in1]
        if not isinstance(scalar, (int, float)):
            rd.append(scalar)
        self.P.add("dve", lambda e: e.scalar_tensor_tensor(out, in0, scalar, in1, op0, op1), rd, [out])

    def cp(self, out, in_, eng="dve"):
        self.P.add(eng, lambda e: e.tensor_copy(out, in_), [in_], [out])

    def dma(self, out, in_, slot, q="sp"):
        rd = [] if str(in_.space) == "DRAM" and in_.tensor.name not in ("s5c", "x1s") else [in_]
        self.P.add(q, lambda e: e.dma_start(out=out, in_=in_), rd, [out], dma=slot)

    def col(self, c, n=1):
        return self.sm[:, c:c + n]

    def slot(self):
        s = self.ps2[self.slot_i % 4]
        self.slot_i += 1
        return s

    def loadw(self, src):
        i = self.wcount
        self.wcount += 1
        st = self.wst[i % self.NS]
        bf = self.wbf[i % self.NB]
        self.dma(st[:], src, f"w{i % self.NS}")
        ce = self.cast_engs[i % len(self.cast_engs)]
        flat = bf[:].rearrange("p a b -> p (a b)")
        if ce == "act":
            self.act(flat, st[:], AF.Copy)
        else:
            self.cp(flat, st[:], eng=ce)
        return bf

    def wstream_begin(self, src, n, look=3):
        self.ws_src = src
        self.ws_n = n
        self.ws_next = 0
        self.ws_fifo = []
        for _ in range(look):
            self._ws_push()

    def _ws_push(self):
        if self.ws_next < self.ws_n:
            self.ws_fifo.append(self.loadw(self.ws_src[self.ws_next]))
            self.ws_next += 1

    def nextw(self):
        wb = self.ws_fifo.pop(0)
        self._ws_push()
        self.wi += 1
        return wb

    def rowstats(self, x, n, mean_col, rstd_col, rms, junk):
        ssq = self.col(self.c_mv)
        ms = self.col(self.c_ms)
        self.act(junk, x, AF.Square, accum=ssq)
        if rms:
            self.act(ms, ssq, AF.Sqrt, scale=1.0 / n, bias=EPS)
        else:
            sm_ = self.col(self.c_mv + 1)
            self.act(junk, x, AF.Identity, accum=sm_)
            self.ts(mean_col, sm_, 1.0 / n, None, ALU.mult)
            t = self.col(self.c_mv + 2)
            self.tt(t, mean_col, mean_col, ALU.mult)
            self.stt(t, ssq, 1.0 / n, t, ALU.mult, ALU.subtract)
            self.act(ms, t, AF.Sqrt, bias=EPS)
        self.P.add("dve", lambda e: e.reciprocal(rstd_col, ms), [ms], [rstd_col])

    def gelu(self, out, in_, tmp):
        if self.gelu_native:
            self.act(out, in_, AF.Gelu_apprx_tanh)
            return
        self.act(tmp, in_, AF.Square)
        self.ts(tmp, tmp, 0.044715, 1.0, ALU.mult, ALU.add)
        self.tt(tmp, tmp, in_, ALU.mult)
        self.act(tmp, tmp, AF.Sigmoid, scale=GELU_C)
        self.tt(out, tmp, in_, ALU.mult)

    def views(self):
        Bb = self.BIGb
        self.uT = Bb[:, 0:8192].rearrange("p (a b) -> p a b", b=512)
        self.vtok = Bb[:, 8192:16384].rearrange("p (a b) -> p a b", b=2048)
        self.xbT = Bb[:, 16384:20480].rearrange("p (a b) -> p a b", b=512)
        self.zT = Bb[:, 20480:24576].rearrange("p (a b) -> p a b", b=512)
        self.s5cb = [Bb[:, 24576:28672], Bb[:, 28672:32768]]
        Bf = self.BIG
        self.tmpF = Bf[:, 12288:14336]
        self.tmpF2 = Bf[:, 14336:16384]
        self.tmpZ = Bf[:, 10240:12288]
        self.vreg = Bf[:, 4096:8192]
        self.mix = Bf[:, :].rearrange("p (a b) -> p a b", b=4096)

    def prologue(self):
        P = self.P
        nc = self.nc
        sm = self.sm
        Bf = self.BIG
        onesf = Bf[:, 0:128]
        idf = Bf[:, 128:256]
        trf = Bf[:, 256:384]
        P.add("pool", lambda e: e.memset(onesf, 1.0), [], [onesf])
        P.add("pool", lambda e: e.affine_select(idf, onesf, [[-1, 128]], ALU.is_equal, 0.0, base=0, channel_multiplier=1), [onesf], [idf])
        P.add("pool", lambda e: e.affine_select(trf, onesf, [[1, 128]], ALU.is_ge, 0.0, base=0, channel_multiplier=-1), [onesf], [trf])
        self.cp(self.ident[:], idf)
        self.cp(self.tri[:], trf)
        self.ts(self.ntri[:], trf, -1.0, None, ALU.mult)
        scol = self.col(self.c_scol)
        P.add("pool", lambda e: e.iota(scol, [[0, 1]], base=0, channel_multiplier=1, allow_small_or_imprecise_dtypes=True), [], [scol])
        trow = self.trow[:]
        P.add("pool", lambda e: e.iota(trow, [[1, 128]], base=0, channel_multiplier=0, allow_small_or_imprecise_dtypes=True), [], [trow])
        self.ts(self.col(self.c_nscol), scol, -1.0, None, ALU.mult)
        self.dma(sm[:, self.c_gcol:self.c_gcol + 64], self.gcol_d[:, :], "misc")
        self.dma(sm[:, self.c_dcol:self.c_dcol + 8], self.dcol_d[:, :], "misc")
        self.dma(sm[:, self.c_lam:self.c_lam + 96], self.lamcol_d[:, :], "misc")
        wsf = Bf[:, 2048:4096]
        self.dma(wsf, self.wsT_d[:, :], "misc")
        wsf3 = wsf.rearrange("p (h t) -> p h t", t=128)
        P.add("pool", lambda e: e.affine_select(wsf3, wsf3, [[0, 16], [1, 128]], ALU.is_ge, 0.0, base=0, channel_multiplier=-1), [wsf3], [wsf3])
        self.cp(self.wsT[:], wsf3)
        car = sm[:, self.c_car:self.c_car + 64]
        P.add("dve", lambda e: e.memset(car, 0.0), [], [car])
        lre = sm[:, self.c_lam:self.c_lam + 32]
        lim = sm[:, self.c_lam + 32:self.c_lam + 64]
        ldt = sm[:, self.c_lam + 64:self.c_lam + 96]
        dtc = sm[:, self.c_dtc:self.c_dtc + 32]
        arc = sm[:, self.c_arc:self.c_arc + 32]
        aic = sm[:, self.c_aic:self.c_aic + 32]
        self.act(dtc, ldt, AF.Exp)
        self.tt(arc, lre, dtc, ALU.mult)
        self.tt(aic, lim, dtc, ALU.mult)
        t0 = sm[:, self.c_t:self.c_t + 32] if False else Bf[:, 384:416]
        t1 = Bf[:, 416:448]
        t2 = Bf[:, 448:480]
        a128re = sm[:, self.c_a128:self.c_a128 + 32]
        a128im = sm[:, self.c_a128 + 32:self.c_a128 + 64]
        self.act(t0, arc, AF.Exp, scale=128.0)
        self.ts(t1, aic, 128.0, None, ALU.mult)
        self.sincos(t1, t2, a128im, a128re, t0)
        if self.NPRE > 0:
            self.prefix_s0_first()
        for k in range(8):
            ks = slice(k * 512, (k + 1) * 512)
            W = Bf[:, 4096:16384]
            lrb = W[:, 0:512]
            lib = W[:, 512:1024]
            ldb = W[:, 1024:1536]
            bre = W[:, 1536:2048]
            bim = W[:, 2048:2560]
            cre_ = W[:, 2560:3584]
            dtb = W[:, 3584:4096]
            arb = W[:, 4096:4608]
            aib = W[:, 4608:5120]
            mag = W[:, 5120:5632]
            sn = W[:, 5632:6144]
            cs = W[:, 6144:6656]
            w0 = W[:, 6656:7168]
            w1 = W[:, 7168:7680]
            w2 = W[:, 7680:8192]
            w3 = W[:, 8192:8704]
            cb = self.xbuf[:, :]
            self.dma(lrb, self.lam_bc[0, :, ks], "pa")
            self.dma(lib, self.lam_bc[1, :, ks], "pb")
            self.dma(ldb, self.lam_bc[2, :, ks], "pc")
            self.dma(W[:, 1536:2560], self.bbd[k, :, :], "pd")
            self.dma(cre_, self.cbd[k, :, :], "pe_")
            self.act(dtb, ldb, AF.Exp)
            self.tt(arb, lrb, dtb, ALU.mult)
            self.tt(aib, lib, dtb, ALU.mult)
            self.act(mag, arb, AF.Exp)
            self.sincos(aib, w2, w0, w1, mag)
            self.ts(w1, w1, -1.0, None, ALU.add)
            self.tt(w2, lrb, lrb, ALU.mult)
            self.tt(w3, lib, lib, ALU.mult)
            self.tt(w2, w2, w3, ALU.add)
            P.add("dve", lambda e, w2=w2: e.reciprocal(w2, w2), [w2], [w2])
            self.tt(sn, w1, lrb, ALU.mult)
            self.tt(w3, w0, lib, ALU.mult)
            self.tt(sn, sn, w3, ALU.add)
            self.tt(sn, sn, w2, ALU.mult)
            self.tt(cs, w0, lrb, ALU.mult)
            self.tt(w3, w1, lib, ALU.mult)
            self.tt(cs, cs, w3, ALU.subtract)
            self.tt(cs, cs, w2, ALU.mult)
            self.tt(w0, bre, sn, ALU.mult)
            self.tt(w1, bim, cs, ALU.mult)
            self.tt(cb[:, 0:512], w0, w1, ALU.subtract)
            self.tt(w0, bre, cs, ALU.mult)
            self.tt(w1, bim, sn, ALU.mult)
            self.tt(cb[:, 512:1024], w0, w1, ALU.add)
            self.act(mag, arb, AF.Exp, scale=self.col(self.c_nscol))
            self.ts(w2, aib, self.col(self.c_scol), None, ALU.mult)
            self.sincos(w2, w3, w0, w1, mag)
            self.cp(cb[:, 1024:1536], w1)
            self.ts(cb[:, 1536:2048], w0, -1.0, None, ALU.mult)
            am = W[:, 8704:9216]
            an = W[:, 9216:9728]
            a0 = W[:, 9728:10240]
            a1 = W[:, 10240:10752]
            a2 = W[:, 10752:11264]
            trow_b = bass.AP(self.trow, 0, [[128, 128], [0, 4], [1, 128]])
            arc_b = bass.AP(self.sm, self.c_arc + 4 * k, [[512, 128], [1, 4], [0, 128]])
            aic_b = bass.AP(self.sm, self.c_aic + 4 * k, [[512, 128], [1, 4], [0, 128]])
            self.tt(am.rearrange("p (a b) -> p a b", b=128), trow_b, arc_b, ALU.mult)
            self.act(am, am, AF.Exp)
            self.tt(an.rearrange("p (a b) -> p a b", b=128), trow_b, aic_b, ALU.mult)
            self.sincos(an, a2, a0, a1, am)
            self.cp(cb[:, 2048:2560], a1)
            self.cp(cb[:, 2560:3072], a0)
            self.cp(cb[:, 3072:3584], cre_[:, 0:512])
            self.ts(cb[:, 3584:4096], cre_[:, 512:1024], -1.0, None, ALU.mult)
            self.dma(self.s5c[k, :, :], cb, "s5w")

    def sincos(self, ang, tmp, out_sin, out_cos, mag):
        ti = tmp.bitcast(mybir.dt.int32)
        self.ts(tmp, ang, 1.0 / (2 * PI), None, ALU.mult)
        self.cp(ti, tmp)
        self.cp(tmp, ti)
        self.stt(tmp, tmp, -2 * PI, ang, ALU.mult, ALU.add)
        self.ts(out_cos, tmp, PI, 2 * PI, ALU.is_gt, ALU.mult)
        self.tt(tmp, tmp, out_cos, ALU.subtract)
        self.ts(out_cos, tmp, -PI, 2 * PI, ALU.is_lt, ALU.mult)
        self.tt(tmp, tmp, out_cos, ALU.add)
        self.ts(tmp, tmp, PI, -PI, ALU.min, ALU.max)
        self.act(out_sin, tmp, AF.Sin)
        self.tt(out_sin, out_sin, mag, ALU.mult)
        self.stt(out_cos, tmp, -1.0, tmp, ALU.mult, ALU.max)
        self.ts(out_cos, out_cos, -1.0, PI / 2, ALU.mult, ALU.add)
        self.act(out_cos, out_cos, AF.Sin)
        self.tt(out_cos, out_cos, mag, ALU.mult)

    def s0_from(self, get_rows, gidx):
        hT = self.actA
        for tb in range(4):
            x = get_rows(tb)
            rstd = self.col(self.c_stat + tb)
            if self.stop == 's0x':
                continue
            self.rowstats(x, D, None, rstd, True, self.xbuf[:, :])
            if self.stop == 's0a':
                continue
            self.ts(self.xbuf[:, :], x, rstd, None, ALU.mult)
            if self.stop == 's0b':
                continue
            self.transposes(tb, gidx)

    def transposes(self, tb, gidx, fixed=None):
        hT = self.actA
        for g8 in range(4):
            psb = (fixed if fixed is not None else self.slot())[:, :].rearrange("p (a b) -> p a b", b=128)
            for j in range(8):
                kt = g8 * 8 + j
                self.mm(psb[:, j, :], self.xbuf[:, kt * 128:(kt + 1) * 128], self.ident[:, :], True, True)
            if self.stop == 's0c':
                continue
            for j in range(8):
                kt = g8 * 8 + j
                if self.stop == 's0d':
                    self.cp(hT[:, kt, tb * 128:(tb + 1) * 128], psb[:, j, :])
                    continue
                if j % 2 == 0:
                    self.act(hT[:, kt, tb * 128:(tb + 1) * 128], psb[:, j, :], AF.Copy,
                             scale=self.col(self.c_gcol + gidx * 32 + kt))
                else:
                    self.ts(hT[:, kt, tb * 128:(tb + 1) * 128], psb[:, j, :],
                            self.col(self.c_gcol + gidx * 32 + kt), None, ALU.mult)

    def s0_tb_alt(self, xsrc, row0, tb, gidx, upper_only=False, use_slot=False):
        stg = self.actB.bitcast(F32)[:].rearrange("p a b -> p (a b)")
        half = 1 if upper_only else tb % 2
        x = stg[:, half * 4096:(half + 1) * 4096]
        self.dma(x, xsrc[row0 + tb * 128:row0 + (tb + 1) * 128, :], f"xa{half}")
        rstd = self.col(self.c_stat + tb)
        self.rowstats(x, D, None, rstd, True, self.xbuf[:, :])
        self.ts(self.xbuf[:, :], x, rstd, None, ALU.mult)
        self.transposes(tb, gidx, fixed=None if use_slot else self.ps2[3])

    def s0(self, xsrc, row0, gidx):
        for tb in range(4):
            self.dma(self.mix[:, tb, :], xsrc[row0 + tb * 128:row0 + (tb + 1) * 128, :], f"x{tb}")
        self.s0_from(lambda tb: self.mix[:, tb, :], gidx)

    def fm_group(self, wbs, acts, ps):
        n = len(wbs)
        for bi, (wb, af) in enumerate(zip(wbs, acts)):
            for kt in range(16):
                for sub in range(2):
                    self.mm(ps[:, sub, :], wb[:, kt, sub * 128:(sub + 1) * 128], af(kt),
                            start=(bi == 0 and kt == 0), stop=(bi == n - 1 and kt == 15))

    def fm(self, srcs, actfn_list):
        ps = self.slot()[:, :].rearrange("p (a b) -> p a b", b=512)
        n = len(srcs)
        for bi in range(n):
            wb = srcs[bi]()
            af = actfn_list[bi]
            for kt in range(KB):
                for sub in range(2):
                    self.mm(ps[:, sub, :], wb[:, kt, sub * 128:(sub + 1) * 128], af(kt),
                            start=(bi == 0 and kt == 0), stop=(bi == n - 1 and kt == KB - 1))
        return ps

    def tm(self, srcs, actfn_list):
        ps = self.slot()[:, :].rearrange("p (a b) -> p a b", b=256)
        n = len(srcs)
        for bi in range(n):
            wb = srcs[bi]()
            af = actfn_list[bi]
            for kt in range(KB):
                for tb in range(4):
                    self.mm(ps[:, tb, :], af(kt, tb), wb[:, kt, :],
                            start=(bi == 0 and kt == 0 and tb % 2 == 0), stop=(bi == n - 1 and kt == KB - 1), skip=True)
        return ps

    @staticmethod
    def kf(T, n):
        return [lambda kt, b=b: T[:, KB * b + kt, :] for b in range(n)]

    @staticmethod
    def kft(T, n):
        return [lambda kt, tb, b=b: T[:, KB * b + kt, tb * 128:(tb + 1) * 128] for b in range(n)]

    def s5_load_consts(self, k):
        cb = self.s5cb[k % 2]
        self.dma(cb, self.s5c[k, :, :], f"s5c{k % 2}")
        return cb

    def s5_x(self, cb, k, c, Xps):
        lhsT = self.xbT[:, k, c * 128:(c + 1) * 128]
        self.mm(Xps[:, 0:512], lhsT, cb[:, 0:512], True, True)
        self.mm(Xps[:, 512:1024], lhsT, cb[:, 512:1024], True, True)

    def s5_pre(self, cb, Xps, Xp=None):
        vr = self.vreg
        if Xp is None:
            Xp = vr[:, 1024:1536].bitcast(BF16)
        t1 = vr[:, 0:1024]
        t2 = self.t2_ap if getattr(self, 't2_ap', None) is not None else self.xbufF[:, 1024:2048]
        t1v = t1.rearrange("p (a b) -> p a b", b=512)
        t2v = t2.rearrange("p (a b) -> p a b", b=512)
        X3 = Xps[:, :].rearrange("p (a b) -> p a b", b=512)
        cbt = cb.tensor
        rowlen = 1
        for d_ in list(cbt.shape)[1:]:
            rowlen *= d_
        Wre_b = bass.AP(cbt, cb.offset + 1024, [[rowlen, 128], [0, 2], [1, 512]])
        Wim_b = bass.AP(cbt, cb.offset + 1536, [[rowlen, 128], [0, 2], [1, 512]])
        self.tt(t1v, X3, Wre_b, ALU.mult)
        self.tt(t2v, X3, Wim_b, ALU.mult)
        self.tt(Xp[:, 0:512], t1[:, 0:512], t2[:, 512:1024], ALU.subtract)
        self.tt(Xp[:, 512:1024], t2[:, 0:512], t1[:, 512:1024], ALU.add)
        return Xp

    def s5_prod(self, cb, Xps, p1, p2):
        X3 = Xps[:, :].rearrange("p (a b) -> p a b", b=512)
        cbt = cb.tensor
        rowlen = 1
        for d_ in list(cbt.shape)[1:]:
            rowlen *= d_
        Wre_b = bass.AP(cbt, cb.offset + 1024, [[rowlen, 128], [0, 2], [1, 512]])
        Wim_b = bass.AP(cbt, cb.offset + 1536, [[rowlen, 128], [0, 2], [1, 512]])
        self.tt(p1.rearrange("p (a b) -> p a b", b=512), X3, Wre_b, ALU.mult)
        self.tt(p2.rearrange("p (a b) -> p a b", b=512), X3, Wim_b, ALU.mult)

    def s5_carry_update(self, k, p127, has_carry=False):
        sm = self.sm
        car = sm[:, self.c_car + 8 * k:self.c_car + 8 * k + 8]
        q = sm[:, self.c_q:self.c_q + 8]
        t = sm[:, self.c_t:self.c_t + 16]
        are = sm[:, self.c_a128 + 4 * k:self.c_a128 + 4 * k + 4]
        aim = sm[:, self.c_a128 + 32 + 4 * k:self.c_a128 + 32 + 4 * k + 4]
        if has_carry:
            q = p127
        else:
            self.tt(q, p127, car, ALU.add)
        self.tt(t[:, 0:4], q[:, 0:4], are, ALU.mult)
        self.tt(t[:, 4:8], q[:, 4:8], aim, ALU.mult)
        self.tt(t[:, 8:12], q[:, 0:4], aim, ALU.mult)
        self.tt(t[:, 12:16], q[:, 4:8], are, ALU.mult)
        self.tt(car[:, 0:4], t[:, 0:4], t[:, 4:8], ALU.subtract)
        self.tt(car[:, 4:8], t[:, 8:12], t[:, 12:16], ALU.add)

    def s5_state_only(self, next_s0=None):
        sm = self.sm
        self.t2_ap = self.vreg[:, 2048:3072]
        Bb = self.BIGb
        fcs = [Bb[:, 0:8192].rearrange("p (k c) -> p k c", c=2048),
               Bb[:, 20480:28672].rearrange("p (k c) -> p k c", c=2048)]
        for h in range(2):
            self.dma(fcs[h], self.s5c[4 * h:4 * h + 4, :, 0:2048].rearrange("k p c -> p k c"), f"s5f{h}")
        vr = self.vreg
        car = sm[:, self.c_car:self.c_car + 64]
        car3 = car.rearrange("p (k c) -> p k c", c=8)
        are = sm[:, self.c_a128:self.c_a128 + 32].rearrange("p (k c) -> p k c", c=4)
        aim = sm[:, self.c_a128 + 32:self.c_a128 + 64].rearrange("p (k c) -> p k c", c=4)
        P1 = [vr[:, 0:512].bitcast(BF16), vr[:, 512:1024].bitcast(BF16)]
        P2 = [vr[:, 1024:1536].bitcast(BF16), vr[:, 1536:2048].bitcast(BF16)]
        q = vr[:, 2048:2112]
        q3 = q.rearrange("p (k c) -> p k c", c=8)
        T = [vr[:, 2112 + 32 * i:2144 + 32 * i].rearrange("p (k c) -> p k c", c=4) for i in range(4)]
        csb = vr[:, 2304:2432]
        A3 = csb[:, 0:64].rearrange("p (k c) -> p k c", c=8)
        B3 = csb[:, 64:128].rearrange("p (k c) -> p k c", c=8)
        xbank = [self.ps2[0], self.ps2[2]]
        steps = [(c, k) for c in range(4) for k in range(8)]
        self.s5_x(fcs[0][:, 0, :], 0, 0, xbank[0])
        for c in range(4):
            cs = self.ps2[1][:, 0:128]
            for k in range(8):
                si = c * 8 + k
                cb = fcs[k // 4][:, k % 4, :]
                Xps = xbank[si % 2]
                if si + 1 < 32:
                    c2, k2 = steps[si + 1]
                    self.s5_x(fcs[k2 // 4][:, k2 % 4, :], k2, c2, xbank[(si + 1) % 2])
                p1 = P1[si % 2]
                p2 = P2[si % 2]
                X3 = Xps[:, :].rearrange("p (a b) -> p a b", b=512)
                cbt = cb.tensor
                rowlen = 1
                for d_ in list(cbt.shape)[1:]:
                    rowlen *= d_
                Wre_b = bass.AP(cbt, cb.offset + 1024, [[rowlen, 128], [0, 2], [1, 512]])
                Wim_b = bass.AP(cbt, cb.offset + 1536, [[rowlen, 128], [0, 2], [1, 512]])
                self.tt(p1.rearrange("p (a b) -> p a b", b=512), X3, Wre_b, ALU.mult)
                self.tt(p2.rearrange("p (a b) -> p a b", b=512), X3, Wim_b, ALU.mult)
                for i in range(8):
                    self.mm(cs[:, 8 * k + i:8 * k + i + 1], p1[:, i * 128:(i + 1) * 128], self.tri[:, 127:128], True, True)
                for i in range(8):
                    self.mm(cs[:, 64 + 8 * k + i:64 + 8 * k + i + 1], p2[:, i * 128:(i + 1) * 128], self.tri[:, 127:128], True, True)
            self.cp(csb, cs)
            self.tt(q3[:, :, 0:4], A3[:, :, 0:4], B3[:, :, 4:8], ALU.subtract)
            self.tt(q3[:, :, 4:8], B3[:, :, 0:4], A3[:, :, 4:8], ALU.add)
            self.tt(q, q, car, ALU.add)
            self.tt(T[0], q3[:, :, 0:4], are, ALU.mult)
            self.tt(T[1], q3[:, :, 4:8], aim, ALU.mult)
            self.tt(T[2], q3[:, :, 0:4], aim, ALU.mult)
            self.tt(T[3], q3[:, :, 4:8], are, ALU.mult)
            self.tt(car3[:, :, 0:4], T[0], T[1], ALU.subtract)
            self.tt(car3[:, :, 4:8], T[2], T[3], ALU.add)
            if next_s0 is not None:
                next_s0(c)
        self.t2_ap = None

    def s5_full(self):
        sm = self.sm
        vr = self.vreg
        xbank = [self.ps2[0], self.ps2[3]]
        P1 = [vr[:, 0:512].bitcast(BF16), vr[:, 512:1024].bitcast(BF16)]
        P2 = [vr[:, 1024:1536].bitcast(BF16), self.gbc[:, 0:512].bitcast(BF16)]
        cbs = {0: self.s5_load_consts(0)}
        self.s5_x(cbs[0], 0, 0, xbank[0])
        self.s5_prod(cbs[0], xbank[0], P1[0], P2[0])
        for k in range(8):
            cb = cbs[k]
            if k + 1 < 8:
                cbs[k + 1] = self.s5_load_consts(k + 1)
            yps = self.ps2[2][:, (k % 2) * 512:(k % 2) * 512 + 512]
            Cm = cb[:, 3072:4096].rearrange("p (a b) -> p a b", b=128)
            for c in range(4):
                si = k * 4 + c
                p1 = P1[si % 2]
                p2 = P2[si % 2]
                if si + 1 < 32:
                    k2, c2 = (si + 1) // 4, (si + 1) % 4
                    self.s5_x(cbs[k2], k2, c2, xbank[(si + 1) % 2])
                PT = self.ps2[1][:, :].rearrange("p (a b) -> p a b", b=128)
                for i in range(4):
                    self.mm(PT[:, i, :], p1[:, i * 128:(i + 1) * 128], self.tri[:, :], True, False)
                    self.mm(PT[:, i, :], p2[:, (4 + i) * 128:(5 + i) * 128], self.ntri[:, :], False, True)
                for i in range(4):
                    self.mm(PT[:, 4 + i, :], p2[:, i * 128:(i + 1) * 128], self.tri[:, :], True, False)
                    self.mm(PT[:, 4 + i, :], p1[:, (4 + i) * 128:(5 + i) * 128], self.tri[:, :], False, True)
                if si + 1 < 32:
                    self.s5_prod(cbs[k2], xbank[(si + 1) % 2], P1[(si + 1) % 2], P2[(si + 1) % 2])
                Ptot = vr[:, 1536:2560]
                t1 = vr[:, 2560:3584]
                t2 = self.xbufF[:, 0:1024]
                sT = vr[:, 3584:4096].bitcast(BF16)
                for i in range(8):
                    ccol = sm[:, self.c_car + 8 * k + i:self.c_car + 8 * k + i + 1]
                    self.act(Ptot[:, i * 128:(i + 1) * 128], PT[:, i, :], AF.Identity, bias=ccol)
                cbt = cb.tensor
                rowlen = 1
                for d_ in list(cbt.shape)[1:]:
                    rowlen *= d_
                Are_b = bass.AP(cbt, cb.offset + 2048, [[rowlen, 128], [0, 2], [1, 512]])
                Aim_b = bass.AP(cbt, cb.offset + 2560, [[rowlen, 128], [0, 2], [1, 512]])
                P3 = Ptot.rearrange("p (a b) -> p a b", b=512)
                self.tt(t1.rearrange("p (a b) -> p a b", b=512), P3, Are_b, ALU.mult)
                self.tt(t2.rearrange("p (a b) -> p a b", b=512), P3, Aim_b, ALU.mult)
                self.tt(sT[:, 0:512], t1[:, 0:512], t2[:, 512:1024], ALU.subtract)
                self.tt(sT[:, 512:1024], t2[:, 0:512], t1[:, 512:1024], ALU.add)
                q127 = bass.AP(Ptot.tensor, Ptot.offset + 127, [[Ptot.ap[0][0], 128], [128, 8]])
                self.s5_carry_update(k, q127, has_carry=True)
                for i in range(8):
                    self.mm(yps[:, c * 128:(c + 1) * 128], Cm[:, i, :], sT[:, i * 128:(i + 1) * 128],
                            start=(i == 0), stop=(i == 7))
            ty = self.xbufF[:, 0:512]
            self.stt(ty, self.xbT[:, k, :], self.col(self.c_dcol + k), yps, ALU.mult, ALU.add)
            self.gelu(self.zT[:, k, :], ty, self.xbufF[:, 512:1024])

    def xb_proj(self, srcfn):
        hT = self.actA
        for g in range(4):
            ps = self.fm([srcfn] * 4, self.kf(hT, 4))
            self.act(self.xbT[:, 2 * g:2 * g + 2, :], ps, AF.Copy)

    def prefix_s0_first(self):
        for tb in range(4):
            self.s0_tb_alt(self.xp, 0, tb, 0)

    def prefix_tile(self, t):
        self.wi = 0
        if t == 0:
            self.wstream_begin(self.wxb, 16)
        self.xb_proj(self.nextw)
        nxt = None
        if t + 1 < self.NPRE:
            self.wstream_begin(self.wxb, 16)
            nxt = lambda c, t=t: self.s0_tb_alt(self.xp, (t + 1) * TT, c, 0)
        elif self.NT > 0:
            nxt = lambda c: self.s0_tb_alt(self.xo, 0, c, 0)
            self.s0_done = True
        self.s5_state_only(nxt)

    def main_tile(self, t):
        P = self.P
        hT = self.actA
        mT = self.actB
        row0 = t * TT
        self.wi = 0
        nw = self.nextw
        self.wstream_begin(self.wall, NBLK)
        if getattr(self, "s0_done", False):
            self.s0_done = False
        else:
            self.s0(self.xo, row0, 0)
        if self.stop in ('s0', 's0a', 's0b', 's0x', 's0c', 's0d'):
            return
        hfa = lambda kt: hT[:, kt, :]
        hfb = lambda kt: hT[:, 16 + kt, :]
        for g in range(8):
            ps = self.fm([nw] * 4, self.kf(hT, 4))
            self.gelu(self.uT[:, 2 * g:2 * g + 2, :], ps, self.xbufF[:, 0:1024].rearrange("p (a b) -> p a b", b=512))
        for g in range(8):
            ps = self.tm([nw] * 4, self.kft(hT, 4))
            self.gelu(self.vtok[:, :, g * 256:(g + 1) * 256], ps,
                      self.xbufF[:, 0:1024].rearrange("p (a b) -> p a b", b=256))
        self.xb_proj(nw)
        if self.stop == 's1':
            return
        self.dma(self.gbc[:, :], self.lngb[0, :, :], "gbc")
        self.dma(self.xbufF[:, :], self.lngb[1, :, :], "xbuf")
        for tb in range(4):
            mean = self.col(self.c_stat + 4 + tb)
            rstd = self.col(self.c_stat + 8 + tb)
            v = self.vtok[:, tb, :]
            self.rowstats(v, AW, mean, rstd, False, self.tmpF)
            self.stt(self.tmpF, v, mean, self.gbc[:, :], ALU.subtract, ALU.mult)
            self.stt(v, self.tmpF, rstd, self.xbufF[:, :], ALU.mult, ALU.add)
        self.dma(self.xbufF[:, :], self.bsp_d[:, :], "xbuf")
        for h in range(16):
            psA = self.ps2[1 + (h % 2)][:, 0:512]
            for tb in range(4):
                self.mm(psA[:, tb * 128:(tb + 1) * 128], self.vtok[:, tb, h * 128:(h + 1) * 128],
                        self.wsT[:, h, :], True, True)
            xbf = self.xbufF
            bias3 = bass.AP(xbf, h * 128, [[2048, 128], [0, 4], [1, 128]])
            ts_ = self.tmpZ[:, 0:512].rearrange("p (a b) -> p a b", b=128)
            self.tt(ts_, psA.rearrange("p (a b) -> p a b", b=128), bias3, ALU.add)
            self.tt(self.uT[:, h, :], self.tmpZ[:, 0:512], self.uT[:, h, :], ALU.mult)
        if self.stop == 's2':
            return
        self.s5_full()
        if self.stop == 's3':
            return
        vr = self.vreg
        tA = vr[:, 0:1024].rearrange("p (a b) -> p a b", b=512)
        tB = vr[:, 1024:2048].rearrange("p (a b) -> p a b", b=512)
        tG = vr[:, 2048:3072].rearrange("p (a b) -> p a b", b=512)
        sT_ = self.uT
        zT = self.zT
        for j in range(16):
            ps = self.fm([nw] * 2, self.kf(sT_, 2))
            self.act(tA, ps, AF.Copy)
            ps = self.fm([nw] * 4, self.kf(hT, 4))
            self.act(tG, ps, AF.Sigmoid)
            self.tt(tA, tA, tG, ALU.mult)
            psa = self.fm([nw], self.kf(zT, 1))
            psb = self.fm([nw], self.kf(zT, 1))
            self.act(tG, psb, AF.Sigmoid)
            self.tt(tB, psa, tG, ALU.mult)
            ps = self.fm([nw] * 4, self.kf(hT, 4))
            self.act(tG, ps, AF.Sigmoid)
            self.tt(tB, tB, tG, ALU.mult)
            self.tt(mT[:, 2 * j:2 * j + 2, :], tA, tB, ALU.add)
        if self.stop == 's4':
            return
        for j in range(16):
            ps = self.tm([nw] * 4, self.kft(mT, 4))
            self.act(self.mix[:, :, j * 256:(j + 1) * 256], ps, AF.Copy)
        for tb in range(4):
            self.rowstats(self.mix[:, tb, :], D, None, self.col(self.c_stat + tb), True, self.xbuf[:, :])
        for half in range(2):
            hs = slice(half * 2048, (half + 1) * 2048)
            self.dma(self.gbc[:, :], self.gains[0, :, hs], "gbc")
            for tb in range(4):
                self.dma(self.xbufF[:, :], self.xo[row0 + tb * 128:row0 + (tb + 1) * 128, hs], "xbuf")
                m = self.mix[:, tb, hs]
                self.stt(m, m, self.col(self.c_stat + tb), self.gbc[:, :], ALU.mult, ALU.mult)
                self.tt(m, m, self.xbufF[:, :], ALU.add)
        for tb in range(4):
            self.dma(self.x1s[row0 + tb * 128:row0 + (tb + 1) * 128, :], self.mix[:, tb, :], f"x1w{tb}")
        self.s0_from(lambda tb: self.mix[:, tb, :], 1)
        if self.stop == 's5':
            return
        aT = self.actB
        ffv = self.mix
        tR = self.xbufF[:, 0:1024].rearrange("p (a b) -> p a b", b=512)
        for c in range(8):
            for g in range(8):
                ps = self.fm([nw] * 4, self.kf(hT, 4))
                self.act(tR, ps, AF.Relu)
                self.tt(aT[:, 2 * g:2 * g + 2, :], tR, tR, ALU.mult)
            for j in range(16):
                ps = self.tm([nw] * 2, self.kft(aT, 2))
                dst = ffv[:, :, j * 256:(j + 1) * 256]
                if c == 0:
                    self.act(dst, ps, AF.Copy)
                else:
                    self.tt(dst, dst, ps, ALU.add)
                if c == 7 and t + 1 < self.NT and j % 4 == 3:
                    self.s0_tb_alt(self.xo, row0 + TT, j // 4, 0, upper_only=True, use_slot=True)
                    self.s0_done = True
        assert self.wi == NBLK, self.wi
        for tb in range(4):
            self.rowstats(ffv[:, tb, :], D, None, self.col(self.c_stat + tb), True, self.xbuf[:, :])
        for tb in range(4):
            for half in range(2):
                hs = slice(half * 2048, (half + 1) * 2048)
                self.dma(self.gbc[:, :], self.gains[1, :, hs], "gbc")
                self.dma(self.xbufF[:, :], self.x1s[row0 + tb * 128:row0 + (tb + 1) * 128, hs], "xbuf")
                m = ffv[:, tb, hs]
                self.stt(m, m, self.col(self.c_stat + tb), self.gbc[:, :], ALU.mult, ALU.mult)
                self.tt(m, m, self.xbufF[:, :], ALU.add)
            self.dma(self.out[row0 + tb * 128:row0 + (tb + 1) * 128, :], ffv[:, tb, :], f"ow{tb}")

    def build(self):
        self.views()
        self.prologue()
        if self.stop == 'pro':
            self.P.finalize()
            return self.nc
        for t in range(self.NPRE):
            self.prefix_tile(t)
        for t in range(self.NT):
            self.main_tile(t)
        if self.stop:
            self.dma(self.dbgA[:, :], self.actA[:].rearrange("p a b -> p (a b)"), "dbgA")
            self.dma(self.dbgB[:, :], self.actB[:].rearrange("p a b -> p (a b)"), "dbgB")
            self.dma(self.dbgBIG[:, :], self.BIG[:, :], "dbgBIG")
        self.P.finalize()
        return self.nc


def _blocks(W):
    K, N = W.shape
    a = W.reshape(K // 1024, 8, 128, N // 256, 256)
    a = a.transpose(3, 0, 2, 1, 4)
    return np.ascontiguousarray(a).reshape(N // 256, K // 1024, 128, 2048)


def _layout_weights(w_in, w_proj_a, w_glu_a, w_glu_b, w_out, w_ff_up, w_ff_down):
    blocks = []
    bu = _blocks(w_in[:, 0:2048])
    bv = _blocks(w_in[:, 2048:4096])
    bx = _blocks(w_in[:, 4096:5120])
    bga = _blocks(w_in[:, 5120:9216])
    bgb = _blocks(w_in[:, 9216:13312])
    for b in (bu, bv, bx):
        blocks.append(b.reshape(-1, 128, 2048))
    bpa = _blocks(w_proj_a)
    gla = _blocks(w_glu_a)
    glb = _blocks(w_glu_b)
    s4 = np.concatenate([bpa, bga, gla, glb, bgb], axis=1)
    blocks.append(s4.reshape(-1, 128, 2048))
    blocks.append(_blocks(w_out).reshape(-1, 128, 2048))
    for c in range(8):
        up = _blocks(w_ff_up[:, c * 2048:(c + 1) * 2048])
        dn = _blocks(w_ff_down[c * 2048:(c + 1) * 2048, :])
        blocks.append(up.reshape(-1, 128, 2048))
        blocks.append(dn.reshape(-1, 128, 2048))
    wall = np.concatenate(blocks, axis=0)
    assert wall.shape[0] == NBLK, wall.shape
    wxb = np.ascontiguousarray(bx.reshape(-1, 128, 2048))
    return wall, wxb


def _layout_small(norm_mix_pre, norm_mix_post, norm_mlp_pre, norm_mlp_post, v_norm_g, v_norm_b,
                  w_spatial, b_spatial, lam_re, lam_im, log_dt, b_re, b_im, c_re, c_im, d_skip):
    f = np.float32
    m = {}
    m["gains"] = np.ascontiguousarray(np.stack([np.broadcast_to(norm_mix_post[0], (128, D)),
                                                np.broadcast_to(norm_mlp_post[0], (128, D))]).astype(f))
    m["gcol"] = np.ascontiguousarray(np.concatenate([norm_mix_pre[0].reshape(32, 128).T,
                                                     norm_mlp_pre[0].reshape(32, 128).T], axis=1).astype(f))
    m["lngb"] = np.ascontiguousarray(np.stack([np.broadcast_to(v_norm_g[0], (128, AW)),
                                               np.broadcast_to(v_norm_b[0], (128, AW))]).astype(f))
    m["wsT"] = np.ascontiguousarray(w_spatial[0].transpose(2, 0, 1).reshape(128, 2048).astype(f))
    m["bsp"] = np.ascontiguousarray(np.broadcast_to(b_spatial[0].reshape(1, 2048), (128, 2048)).astype(f))
    ldt = np.repeat(log_dt[0][:, None], 64, axis=1)
    flat = np.stack([lam_re[0].reshape(-1), lam_im[0].reshape(-1), ldt.reshape(-1)])
    m["lam_bc"] = np.ascontiguousarray(np.broadcast_to(flat[:, None, :], (3, 128, 4096)).astype(f))
    m["lamcol"] = np.ascontiguousarray(np.concatenate([flat[i].reshape(32, 128).T for i in range(3)], axis=1).astype(f))
    bbd = np.zeros((8, 128, 1024), f)
    cbd = np.zeros((8, 128, 1024), f)
    for k in range(8):
        for gp in range(8):
            g = 8 * k + gp
            bbd[k, 16 * gp:16 * gp + 16, 64 * gp:64 * gp + 64] = b_re[0, g].T
            bbd[k, 16 * gp:16 * gp + 16, 512 + 64 * gp:512 + 64 * gp + 64] = b_im[0, g].T
            i, g2 = gp // 2, gp % 2
            cbd[k, g2 * 64:g2 * 64 + 64, i * 128 + 16 * gp:i * 128 + 16 * gp + 16] = c_re[0, g].T
            cbd[k, g2 * 64:g2 * 64 + 64, 512 + i * 128 + 16 * gp:512 + i * 128 + 16 * gp + 16] = c_im[0, g].T
    m["bbd"] = bbd
    m["cbd"] = cbd
    m["dcol"] = np.ascontiguousarray(d_skip[0].reshape(8, 128).T.astype(f))
    return m


_NC_CACHE = {}


def kernel(x, norm_mix_pre, w_in, v_norm_g, v_norm_b, w_spatial, b_spatial, w_proj_a,
           lam_re, lam_im, log_dt, b_re, b_im, c_re, c_im, d_skip, w_glu_a, w_glu_b,
           w_out, norm_mix_post, norm_mlp_pre, w_ff_up, w_ff_down, norm_mlp_post):
    x = np.asarray(x, np.float32)
    A = lambda a: np.asarray(a, np.float32)
    wall, wxb = _layout_weights(A(w_in)[0], A(w_proj_a)[0], A(w_glu_a)[0], A(w_glu_b)[0],
                                A(w_out)[0], A(w_ff_up)[0], A(w_ff_down)[0])
    small = _layout_small(A(norm_mix_pre), A(norm_mix_post), A(norm_mlp_pre), A(norm_mlp_post),
                          A(v_norm_g), A(v_norm_b), A(w_spatial), A(b_spatial), A(lam_re), A(lam_im),
                          A(log_dt), A(b_re), A(b_im), A(c_re), A(c_im), A(d_skip))
    NPRE = 12
    nc = Builder(NPRE=NPRE, NT=4).build()
    in_maps = []
    for c in range(NCORES):
        b, q = c // 4, c % 4
        xo = np.ascontiguousarray(x[b, q * OWN:(q + 1) * OWN])
        xp = np.zeros((NPRE * TT, D), np.float32)
        if q > 0:
            xp[NPRE * TT - q * OWN:] = x[b, 0:q * OWN]
        d = {"xo": xo, "xp": xp, "wall": wall, "wxb": wxb}
        d.update(small)
        in_maps.append(d)
    res = run_bass_kernel_spmd(nc, in_maps, core_ids=list(range(NCORES)))
    out = np.zeros((2, 8192, D), np.float32)
    for c in range(NCORES):
        b, q = c // 4, c % 4
        out[b, q * OWN:(q + 1) * OWN] = res.results[c]["out"]
    return out
```
